# Optimizing a Trainium2 kernel written in Bass

```python
import jax, jax.numpy as jnp
from jax import lax
import numpy as np

D_MODEL = 1024
BATCH = 8
SEQ = 4096
DEPTH = 2

HEAD_DIM = 64
N_RET_HEADS = 6
N_ATTN_HEADS = 6
N_GMLP_GROUPS = 4
GMLP_GROUP_DIM = 64
RET_WIDTH = N_RET_HEADS * HEAD_DIM
ATTN_WIDTH = N_ATTN_HEADS * HEAD_DIM
GMLP_WIDTH = N_GMLP_GROUPS * GMLP_GROUP_DIM
MIX_WIDTH = RET_WIDTH + ATTN_WIDTH + GMLP_WIDTH
IN_PROJ_WIDTH = 4 * RET_WIDTH + 3 * ATTN_WIDTH + 2 * GMLP_WIDTH
RET_CHUNK = 128
GMLP_CHUNK = 128
DILATED_PATTERNS = ((128, 1), (512, 4), (2048, 16))
ROPE_THETA = 10000.0
FFN_HIDDEN = ((8 * D_MODEL + 3 * 256 - 1) // (3 * 256)) * 256
EPS = 1e-6

kernel_name = "hybrid_retention_dilated_gmlp_block"


def rms_norm(x, g):
    xf = x.astype(jnp.float32)
    y = xf * lax.rsqrt(jnp.mean(xf * xf, axis=-1, keepdims=True) + EPS)
    return (y * g.astype(jnp.float32)).astype(x.dtype)


def rope_tables(positions):
    inv_freq = ROPE_THETA ** (-jnp.arange(0, HEAD_DIM, 2, dtype=jnp.float32) / HEAD_DIM)
    ang = positions.astype(jnp.float32)[..., None] * inv_freq
    ang = jnp.concatenate([ang, ang], axis=-1)[:, None]
    return jnp.cos(ang), jnp.sin(ang)


def apply_rope(t, cos, sin):
    half = t.shape[-1] // 2
    rot = jnp.concatenate([-t[..., half:], t[..., :half]], axis=-1)
    return t * cos + rot * sin


def split_heads(t, n_heads):
    b, s, _ = t.shape
    return t.reshape(b, s, n_heads, -1).transpose(0, 2, 1, 3)


def merge_heads(t):
    b, h, s, d = t.shape
    return t.transpose(0, 2, 1, 3).reshape(b, s, h * d)


def retention(q, k, v):
    b, h, s, d = q.shape
    c = RET_CHUNK
    nc = s // c
    log_gamma = jnp.log(1.0 - 2.0 ** (-5.0 - jnp.arange(h, dtype=jnp.float32)))
    k = k * (d ** -0.5)
    qc = q.reshape(b, h, nc, c, d)
    kc = k.reshape(b, h, nc, c, d)
    vc = v.reshape(b, h, nc, c, d)
    idx = jnp.arange(c, dtype=jnp.float32)
    diff = idx[:, None] - idx[None, :]
    decay_mask = jnp.where(diff >= 0.0,
                           jnp.exp(log_gamma[:, None, None] * jnp.maximum(diff, 0.0)[None]), 0.0)
    scores = jnp.einsum('bhnqd,bhnkd->bhnqk', qc, kc) * decay_mask[None, :, None]
    y_inner = jnp.einsum('bhnqk,bhnkd->bhnqd', scores, vc)
    zeta = jnp.exp(log_gamma[:, None] * (c - 1.0 - idx)[None])
    xi = jnp.exp(log_gamma[:, None] * (idx + 1.0)[None])
    chunk_inc = jnp.einsum('bhnkd,bhnke->bhnde', kc * zeta[None, :, None, :, None], vc)
    chunk_decay = jnp.exp(log_gamma * c)[None, :, None, None]

    def step(state, inc):
        return state * chunk_decay + inc, state

    _, prev = lax.scan(step, jnp.zeros((b, h, d, d), jnp.float32), jnp.moveaxis(chunk_inc, 2, 0))
    prev = jnp.moveaxis(prev, 0, 2)
    y_cross = jnp.einsum('bhnqd,bhnde->bhnqe', qc, prev) * xi[None, :, None, :, None]
    y = (y_inner + y_cross).reshape(b, h, s, d)
    mu = jnp.mean(y, axis=-1, keepdims=True)
    var = jnp.mean(jnp.square(y - mu), axis=-1, keepdims=True)
    return (y - mu) * lax.rsqrt(var + EPS)


def dilated_window_branch(q, k, v, window, dilation):
    b, h, s, d = q.shape
    block = window // dilation
    m_len = s // dilation
    nb = -(-m_len // block)
    mp = nb * block

    def to_blocks(t):
        t = t.reshape(b, h, m_len, dilation, d).transpose(0, 1, 3, 2, 4)
        t = jnp.pad(t, ((0, 0), (0, 0), (0, 0), (0, mp - m_len), (0, 0)))
        return t.reshape(b, h, dilation, nb, block, d)

    def with_prev(t):
        prev = jnp.pad(t, ((0, 0), (0, 0), (0, 0), (1, 0), (0, 0), (0, 0)))[:, :, :, :-1]
        return jnp.concatenate([prev, t], axis=-2)

    qb = to_blocks(q)
    kk = with_prev(to_blocks(k))
    vv = with_prev(to_blocks(v))
    sc = jnp.einsum('bhrnqd,bhrnkd->bhrnqk', qb, kk) * (d ** -0.5)
    qi = jnp.arange(block)[:, None]
    kj = jnp.arange(2 * block)[None, :]
    dist = block + qi - kj
    valid = (dist >= 0) & (dist <= block)
    first = (jnp.arange(nb)[:, None, None] == 0) & (kj[None] < block)
    mask = valid[None] & ~first
    sc = jnp.where(mask, sc, -jnp.inf)
    mx = jnp.max(sc, axis=-1, keepdims=True)
    p = jnp.exp(sc - mx)
    denom = jnp.sum(p, axis=-1, keepdims=True)
    o = jnp.einsum('bhrnqk,bhrnkd->bhrnqd', p, vv) / denom
    lse = (mx + jnp.log(denom))[..., 0]
    o = o.reshape(b, h, dilation, mp, d)[:, :, :, :m_len].transpose(0, 1, 3, 2, 4).reshape(b, h, s, d)
    lse = lse.reshape(b, h, dilation, mp)[:, :, :, :m_len].transpose(0, 1, 3, 2).reshape(b, h, s)
    return o, lse


def dilated_attention(q, k, v):
    outs, lses = [], []
    for window, dilation in DILATED_PATTERNS:
        o, lse = dilated_window_branch(q, k, v, window, dilation)
        outs.append(o)
        lses.append(lse)
    w = jax.nn.softmax(jnp.stack(lses, axis=0), axis=0)
    return jnp.sum(w[..., None] * jnp.stack(outs, axis=0), axis=0)


def spatial_gating(u, v, w_s, b_s):
    b, s, _ = v.shape
    c = GMLP_CHUNK
    nc = s // c
    vg = v.reshape(b, nc, c, N_GMLP_GROUPS, GMLP_GROUP_DIM)
    mu = jnp.mean(vg, axis=-1, keepdims=True)
    var = jnp.mean(jnp.square(vg - mu), axis=-1, keepdims=True)
    vg = (vg - mu) * lax.rsqrt(var + EPS)
    tril = jnp.tril(jnp.ones((c, c), jnp.float32))
    w = w_s.astype(jnp.float32) * tril[None]
    mixed = jnp.einsum('gts,bnsgd->bntgd', w, vg) + b_s.astype(jnp.float32).T[None, None, :, :, None]
    return u * mixed.reshape(b, s, GMLP_WIDTH)


def hybrid_layer(x, cos, sin, g_mix_pre, g_mix_post, w_in, w_s, b_s, w_out,
                 g_ffn_pre, g_ffn_post, w_gate_up, w_down):
    h = rms_norm(x, g_mix_pre)
    z = jnp.matmul(h, w_in).astype(jnp.float32)
    cuts = np.cumsum([RET_WIDTH] * 4 + [ATTN_WIDTH] * 3 + [GMLP_WIDTH])
    rq, rk, rv, rg, aq, ak, av, gu, gv = jnp.split(z, cuts, axis=-1)
    rq = apply_rope(split_heads(rq, N_RET_HEADS), cos, sin)
    rk = apply_rope(split_heads(rk, N_RET_HEADS), cos, sin)
    y_ret = merge_heads(retention(rq, rk, split_heads(rv, N_RET_HEADS)))
    y_ret = jax.nn.silu(rg) * y_ret
    aq = apply_rope(split_heads(aq, N_ATTN_HEADS), cos, sin)
    ak = apply_rope(split_heads(ak, N_ATTN_HEADS), cos, sin)
    y_att = merge_heads(dilated_attention(aq, ak, split_heads(av, N_ATTN_HEADS)))
    y_gm = spatial_gating(jax.nn.gelu(gu), jax.nn.gelu(gv), w_s, b_s)
    y = jnp.concatenate([y_ret, y_att, y_gm], axis=-1).astype(x.dtype)
    x = x + rms_norm(jnp.matmul(y, w_out), g_mix_post)
    h = rms_norm(x, g_ffn_pre)
    gate, up = jnp.split(jnp.matmul(h, w_gate_up), [FFN_HIDDEN], axis=-1)
    f = jnp.matmul(jax.nn.silu(gate) * up, w_down)
    return x + rms_norm(f, g_ffn_post)


def setup_inputs(seed: int = 0) -> dict:
    key = jax.random.key(seed)
    ks = jax.random.split(key, 12)
    f32 = jnp.float32
    x = jax.random.normal(ks[0], (BATCH, SEQ, D_MODEL), f32)
    positions = jnp.broadcast_to(jnp.arange(SEQ, dtype=jnp.int32), (BATCH, SEQ))
    gain = lambda k: 1.0 + 0.05 * jax.random.normal(k, (DEPTH, D_MODEL), f32)
    return {
        "x": x,
        "positions": positions,
        "g_mix_pre": gain(ks[1]),
        "g_mix_post": gain(ks[2]),
        "w_in": jax.random.normal(ks[3], (DEPTH, D_MODEL, IN_PROJ_WIDTH), f32) * D_MODEL ** -0.5,
        "w_s": jax.random.normal(ks[4], (DEPTH, N_GMLP_GROUPS, GMLP_CHUNK, GMLP_CHUNK), f32) * GMLP_CHUNK ** -0.5,
        "b_s": 1.0 + 0.1 * jax.random.normal(ks[5], (DEPTH, N_GMLP_GROUPS, GMLP_CHUNK), f32),
        "w_out": jax.random.normal(ks[6], (DEPTH, MIX_WIDTH, D_MODEL), f32) * MIX_WIDTH ** -0.5,
        "g_ffn_pre": gain(ks[7]),
        "g_ffn_post": gain(ks[8]),
        "w_gate_up": jax.random.normal(ks[9], (DEPTH, D_MODEL, 2 * FFN_HIDDEN), f32) * D_MODEL ** -0.5,
        "w_down": jax.random.normal(ks[10], (DEPTH, FFN_HIDDEN, D_MODEL), f32) * FFN_HIDDEN ** -0.5,
    }


def reference(x, positions, g_mix_pre, g_mix_post, w_in, w_s, b_s, w_out,
              g_ffn_pre, g_ffn_post, w_gate_up, w_down):
    cos, sin = rope_tables(positions)
    for l in range(DEPTH):
        x = hybrid_layer(x, cos, sin, g_mix_pre[l], g_mix_post[l], w_in[l], w_s[l], b_s[l], w_out[l],
                         g_ffn_pre[l], g_ffn_post[l], w_gate_up[l], w_down[l])
    return x
```

```python
import contextlib
import math
import numpy as np
import ml_dtypes
import concourse.bass as bass
import concourse.mybir as mybir
from concourse.bass_utils import run_bass_kernel_spmd

F32 = mybir.dt.float32
BF16 = mybir.dt.bfloat16
I32 = mybir.dt.int32
AF = mybir.ActivationFunctionType
ALU = mybir.AluOpType
AX = mybir.AxisListType

D = 1024
S = 4096
NT = 32
L = 2
EPS = 1e-6
FH = 2816
ENGS = ("sync", "scalar", "vector", "gpsimd", "tensor")


class Prog:
    def __init__(self, nc):
        self.nc = nc
        self.ops = []
        self.last_w = {}
        self.readers = {}
        self.dma_cnt = {}
        self.dma_seen = {}
        self.last_on_eng = {}
        self.bar_ops = set()
        self.bar_dma = {}
        self.bar_pending = set()

    def add(self, eng, fn, reads=(), writes=(), dma=None):
        deps = set()
        for r in reads:
            if r in self.last_w:
                deps.add(self.last_w[r])
        for w in writes:
            if w in self.last_w:
                deps.add(self.last_w[w])
            for x in self.readers.get(w, ()):
                deps.add(x)
        dmadeps = {}
        if eng in self.bar_pending:
            self.bar_pending.discard(eng)
            deps |= self.bar_ops
            for k, v in self.bar_dma.items():
                dmadeps[k] = v
        idx = len(self.ops)
        op = dict(eng=eng, fn=fn, deps=deps, dma=dma, dmawait=0, sig=False, dmadeps=dmadeps)
        for d in list(deps):
            dop = self.ops[d]
            if dop["dma"] is not None:
                k = dop["dma"]
                if dma is not None and k == dma:
                    deps.discard(d)
                    continue
                self.dma_seen[k] = True
                dmadeps[k] = max(dmadeps.get(k, 0), self.dma_cnt[k] * 16)
            elif dop["eng"] == "tensor" and eng == "tensor" and dma is None:
                deps.discard(d)
            else:
                dop["sig"] = True
        if dma is not None:
            c = self.dma_cnt.get(dma, 0)
            if self.dma_seen.get(dma, False):
                op["dmawait"] = c * 16
            self.dma_cnt[dma] = c + 1
            self.dma_seen[dma] = False
        self.ops.append(op)
        if dma is None:
            self.last_on_eng[eng] = idx
        for r in reads:
            self.readers.setdefault(r, []).append(idx)
        for w in writes:
            self.last_w[w] = idx
            self.readers[w] = []
        return idx

    def op(self, eng, name, reads=(), writes=(), **kw):
        return self.add(eng, lambda e: getattr(e, name)(**kw), reads, writes)

    def dma(self, eng, key, out, in_, reads=(), writes=()):
        return self.add(eng, lambda e: e.dma_start(out=out, in_=in_), reads, writes, dma=key)

    def barrier(self):
        self.bar_ops = set(self.last_on_eng.values())
        self.bar_dma = {k: c * 16 for k, c in self.dma_cnt.items()}
        for k in self.dma_cnt:
            self.dma_seen[k] = True
        self.bar_pending = set(ENGS)

    def emit(self):
        nc = self.nc
        cnt = {e: 0 for e in ENGS}
        for op in self.ops:
            if op["dma"] is None and op["sig"]:
                cnt[op["eng"]] += 1
                op["sigval"] = cnt[op["eng"]]
        streams = {e: [] for e in ENGS}
        for i, op in enumerate(self.ops):
            streams[op["eng"]].append(i)
        dmakeys = sorted(self.dma_cnt.keys())
        ops = self.ops
        with contextlib.ExitStack() as st:
            esem = {e: st.enter_context(nc.semaphore("s_" + e)) for e in ENGS}
            dsem = {k: st.enter_context(nc.semaphore("d_" + k)) for k in dmakeys}
            block = st.enter_context(nc.Block())

            def run(ename):
                def body(eng):
                    waited = {}

                    def wait(sem, key, val):
                        if val > waited.get(key, 0):
                            eng.wait_ge(sem, val)
                            waited[key] = val

                    for i in streams[ename]:
                        op = ops[i]
                        for d in sorted(op["deps"]):
                            dop = ops[d]
                            if dop["dma"] is None:
                                wait(esem[dop["eng"]], "e" + dop["eng"], dop["sigval"])
                        for k, v in op["dmadeps"].items():
                            wait(dsem[k], "d" + k, v)
                        if op["dma"] is not None and op["dmawait"]:
                            wait(dsem[op["dma"]], "d" + op["dma"], op["dmawait"])
                        ins = op["fn"](eng)
                        if op["dma"] is not None:
                            ins.then_inc(dsem[op["dma"]], 16)
                        elif op["sig"]:
                            ins.then_inc(esem[ename], 1)
                    if ename == "sync":
                        for k in dmakeys:
                            wait(dsem[k], "d" + k, self.dma_cnt[k] * 16)
                return body

            block.sync(run("sync"))
            block.scalar(run("scalar"))
            block.vector(run("vector"))
            block.gpsimd(run("gpsimd"))
            block.tensor(run("tensor"))


class Arena:
    def __init__(self, ap, n):
        self.ap = ap
        self.n = n
        self.off = 0

    def alloc(self, free_shape, dt):
        nel = int(np.prod(free_shape))
        n16 = nel * 2 if dt == F32 else nel
        if self.off % 2:
            self.off += 1
        assert self.off + n16 <= self.n, ("arena overflow", self.off, n16, self.n)
        sl = self.ap[:, self.off:self.off + n16]
        self.off += n16
        if dt == F32:
            sl = sl.bitcast(F32)
        if len(free_shape) == 2:
            sl = sl.rearrange("p (a b) -> p a b", a=free_shape[0])
        elif len(free_shape) == 3:
            sl = sl.rearrange("p (a b c) -> p a b c", a=free_shape[0], b=free_shape[1])
        return sl

    def reset(self, off=0):
        self.off = off


def bc(ap, shape, axes):
    for a in axes:
        ap = ap.unsqueeze(a)
    return ap.to_broadcast(list(shape))


def build_nc(n_layers=L, stop_after=None, debug=False, ntiles=NT, p1_groups=8):
    nc = bass.Bass("TRN2", target_bir_lowering=False)
    dt_in = lambda name, shape, dt: nc.dram_tensor(name, list(shape), dt, kind="ExternalInput").ap()
    sbn = lambda n: "sb_" + n
    x_in = dt_in("x", [S, D], F32)
    pos_in = dt_in("pos", [128, NT], I32)
    gpre_in = dt_in("gpre", [L, 2, 128, 8], F32)
    gpost_in = dt_in("gpost", [L, 2, 128, D], F32)
    w_in_d = dt_in("w_in", [L, D, 3200], F32)
    wsT_d = dt_in("wsT", [L, 128, 4, 128], F32)
    bsT_d = dt_in("bsT", [L, 128, 4], F32)
    w_out_d = dt_in("w_out", [L, D, D], F32)
    w_gu_d = dt_in("w_gu", [L, D, 2 * FH], F32)
    w_dn_d = dt_in("w_dn", [L, FH, D], F32)
    ident_d = dt_in("ident", [128, 128], BF16)
    amask_d = dt_in("amask", [128, 256], BF16)
    cmask_d = dt_in("cmask", [128, 128], BF16)
    invf_d = dt_in("invf", [128, 32], F32)
    rtab_d = dt_in("rtab", [128, 21], F32)
    out_d = nc.dram_tensor("out", [S, D], F32, kind="ExternalOutput").ap()
    skind = "ExternalOutput" if debug else "Internal"
    xs_d = nc.dram_tensor("xs", [S, D], F32, kind=skind).ap()
    vs_d = nc.dram_tensor("vs", [S, 384], BF16, kind=skind).ap()
    yT_d = nc.dram_tensor("yTs", [D, S], BF16, kind=skind).ap()

    WN = 67584
    KN = 15360
    with contextlib.ExitStack() as st:
        sbt = lambda name, shape, dt: st.enter_context(nc.sbuf_tensor("sb_" + name, list(shape), dt))
        pst = lambda name, shape, dt: st.enter_context(nc.psum_tensor(name, list(shape), dt))
        arW_t = sbt("arW", [128, WN], BF16)
        arK_t = sbt("arK", [128, KN], BF16)
        ident = sbt("ident", [128, 128], BF16)
        amask = sbt("amask", [128, 2, 128], BF16)
        cmask = sbt("cmask", [128, 128], BF16)
        ones = sbt("ones", [128, 64], BF16)
        invf = sbt("invf", [128, 32], F32)
        rtab = sbt("rtab", [128, 21], F32)
        cos_t = sbt("cos_t", [128, NT, 32], F32)
        sin_t = sbt("sin_t", [128, NT, 32], F32)
        nsin_t = sbt("nsin_t", [128, NT, 32], F32)
        pos_i = sbt("pos_i", [128, NT], I32)
        gpre = sbt("gpre", [128, 2, 8], F32)
        gpost = sbt("gpost", [128, D], F32)
        wsTb = sbt("wsTb", [128, 4, 128], BF16)
        bsh = sbt("bsh", [128, 4], F32)
        mhalf = sbt("mhalf", [128, 8], F32)
        stat = sbt("stat", [128, 64], F32)
        S_f = sbt("S_f", [128, 3, 64], F32)
        S_b = sbt("S_b", [128, 3, 64], BF16)
        xb = [sbt("xb0", [128, D], F32), sbt("xb1", [128, D], F32)]

        T0 = pst("T0", [128, 1024], BF16)
        T1 = pst("T1", [128, 1024], BF16)
        Dp = pst("Dp", [128, 1024], F32)
        Fp = [pst("F%d" % i, [128, 512], F32) for i in range(4)]

        arW = Arena(arW_t[:], WN)
        arK = Arena(arK_t[:], KN)
        P = Prog(nc)
        pool = [(Fp[0][:], "F0"), (Fp[1][:], "F1"), (Fp[2][:], "F2"), (Fp[3][:], "F3"),
                (Dp[:, 0:512], "Dlo"), (Dp[:, 512:1024], "Dhi")]
        pool_i = [0]

        def next_ps():
            r = pool[pool_i[0] % len(pool)]
            pool_i[0] += 1
            return r

        XI = rtab[:, 0:6]
        KS1 = rtab[:, 6:12]
        KS2 = rtab[:, 12:18]
        DEC = rtab[:, 18:21]

        P.dma("sync", "c0", ident[:], ident_d, writes=["ident"])
        P.dma("sync", "c0", amask[:], amask_d.rearrange("p (a b) -> p a b", a=2), writes=["amask"])
        P.dma("sync", "c0", cmask[:], cmask_d, writes=["cmask"])
        P.dma("sync", "c0", invf[:], invf_d, writes=["invf"])
        P.dma("sync", "c0", rtab[:], rtab_d, writes=["rtab"])
        P.dma("sync", "c0", pos_i[:], pos_in, writes=["pos_i"])
        P.op("gpsimd", "memset", writes=["mhalf"], ap=mhalf[:], constant=-0.5)
        P.op("gpsimd", "memset", writes=["ones"], ap=ones[:], constant=1.0)
        arK.reset()
        pos_f = arK.alloc([NT], F32)
        ang = arK.alloc([NT, 32], F32)
        kf = arK.alloc([NT, 32], F32)
        rr = arK.alloc([NT, 32], F32)
        mm = arK.alloc([NT, 32], F32)
        k_i3 = arK.alloc([NT, 32], F32).bitcast(I32)
        TWO_PI = 2.0 * math.pi
        C1 = 6.28125
        C2 = TWO_PI - C1
        PI_LO = 3.1415925
        P.op("vector", "tensor_copy", reads=["pos_i"], writes=["pos_f"], out=pos_f, in_=pos_i[:])
        P.op("vector", "tensor_tensor", reads=["pos_f", "invf"], writes=["ang"], out=ang,
             in0=bc(pos_f, [128, NT, 32], [2]), in1=bc(invf[:], [128, NT, 32], [1]), op=ALU.mult)
        P.op("vector", "tensor_scalar", reads=["ang"], writes=["k_i"], out=k_i3, in0=ang, scalar1=1.0 / TWO_PI,
             scalar2=None, op0=ALU.mult)
        P.op("vector", "tensor_copy", reads=["k_i"], writes=["kf"], out=kf, in_=k_i3)
        P.op("vector", "scalar_tensor_tensor", reads=["kf", "ang"], writes=["rr"], out=rr, in0=kf, scalar=-C1, in1=ang,
             op0=ALU.mult, op1=ALU.add)
        P.op("vector", "scalar_tensor_tensor", reads=["kf", "rr"], writes=["rr"], out=rr, in0=kf, scalar=-C2, in1=rr,
             op0=ALU.mult, op1=ALU.add)
        P.op("vector", "tensor_scalar", reads=["rr"], writes=["rr"], out=rr, in0=rr, scalar1=PI_LO, scalar2=-PI_LO,
             op0=ALU.min, op1=ALU.max)
        P.op("scalar", "activation", reads=["rr"], writes=["sin_t"], out=sin_t[:], in_=rr, func=AF.Sin)
        P.op("vector", "tensor_scalar", reads=["sin_t"], writes=["nsin_t"], out=nsin_t[:], in0=sin_t[:], scalar1=-1.0,
             scalar2=None, op0=ALU.mult)
        P.op("vector", "tensor_scalar", reads=["rr"], writes=["kf"], out=kf, in0=rr, scalar1=0.5 * math.pi, scalar2=None,
             op0=ALU.add)
        P.op("vector", "tensor_scalar", reads=["kf"], writes=["mm"], out=mm, in0=kf, scalar1=math.pi, scalar2=None,
             op0=ALU.is_gt)
        P.op("vector", "scalar_tensor_tensor", reads=["mm", "kf"], writes=["kf"], out=kf, in0=mm, scalar=-TWO_PI, in1=kf,
             op0=ALU.mult, op1=ALU.add)
        P.op("vector", "tensor_scalar", reads=["kf"], writes=["kf"], out=kf, in0=kf, scalar1=PI_LO, scalar2=-PI_LO,
             op0=ALU.min, op1=ALU.max)
        P.op("scalar", "activation", reads=["kf"], writes=["cos_t"], out=cos_t[:], in_=kf, func=AF.Sin)
        P.barrier()
        if stop_after == (0, 0):
            P.dma("sync", "dbg", xs_d[0:128, 0:32], cos_t[:, 0, :], reads=["cos_t"])
            P.dma("sync", "dbg", xs_d[0:128, 32:64], sin_t[:, 0, :], reads=["sin_t"])
            P.dma("sync", "dbg", xs_d[128:256, 0:32], cos_t[:, 31, :], reads=["cos_t"])
            P.dma("sync", "dbg", xs_d[128:256, 32:64], sin_t[:, 31, :], reads=["sin_t"])
            n_layers = 0

        def rms_rstd(src, src_res, col, junk, junk_res):
            P.op("scalar", "activation", reads=[src_res], writes=[junk_res, "st%d" % col], out=junk, in_=src,
                 func=AF.Square, accum_out=stat[:, col:col + 1])
            P.op("vector", "tensor_scalar", reads=["st%d" % col], writes=["st%d" % (col + 1)],
                 out=stat[:, col + 1:col + 2], in0=stat[:, col:col + 1], scalar1=1.0 / D, scalar2=EPS,
                 op0=ALU.mult, op1=ALU.add)
            P.op("gpsimd", "tensor_tensor", reads=["st%d" % (col + 1), "mhalf"], writes=["st%d" % (col + 2)],
                 out=stat[:, col + 2:col + 3], in0=stat[:, col + 1:col + 2], in1=mhalf[:, 0:1], op=ALU.pow)
            return stat[:, col + 2:col + 3], "st%d" % (col + 2)

        def make_hT(xt, xres, gsel, hT, hres):
            rstd, rres = rms_rstd(xt, xres, 0, xn, "xn")
            P.op("vector", "tensor_scalar", reads=[xres, rres], writes=["xn"], out=xn, in0=xt, scalar1=rstd, scalar2=None,
                 op0=ALU.mult)
            for c in range(8):
                P.op("tensor", "transpose", reads=["xn", "ident"], writes=["T0"], out=T0[:, c * 128:(c + 1) * 128],
                     in_=xn[:, c * 128:(c + 1) * 128], identity=ident[:])
            P.op("vector", "tensor_tensor", reads=["T0", "gpre"], writes=[hres], out=hT,
                 in0=T0[:].rearrange("p (a b) -> p a b", a=8), in1=bc(gpre[:, gsel, :], [128, 8, 128], [2]), op=ALU.mult)

        def post_norm_residual(xt, xres):
            rstd, rres = rms_rstd(Dp[:], "Dp", 4, ptmp.bitcast(BF16)[:, 0:D], "ptmp")
            P.op("vector", "scalar_tensor_tensor", reads=["Dp", rres, "gpost"], writes=["ptmp"], out=ptmp, in0=Dp[:],
                 scalar=rstd, in1=gpost[:], op0=ALU.mult, op1=ALU.mult)
            P.op("gpsimd", "tensor_tensor", reads=["ptmp", xres], writes=[xres], out=xt, in0=xt, in1=ptmp, op=ALU.add)

        def load_layer_consts(l):
            P.dma("sync", "c1", gpre[:, 0, :], gpre_in[l, 0], writes=["gpre"])
            P.dma("sync", "c1", gpre[:, 1, :], gpre_in[l, 1], writes=["gpre"])

        x_src = x_in
        for l in range(n_layers):
            load_layer_consts(l)
            arW.reset()
            arK.reset()
            Win = arW.alloc([8, 3200], BF16)
            aqT = arW.alloc([3, S], BF16)
            akT = arW.alloc([3, S], BF16)
            xn = arK.alloc([D], BF16)
            hTb = [arK.alloc([8, 128], BF16), arK.alloc([8, 128], BF16)]
            ytile = arK.alloc([D], BF16)
            qb = arK.alloc([384], BF16)
            kb1 = arK.alloc([384], BF16)
            kb2 = arK.alloc([384], BF16)
            vb = arK.alloc([384], BF16)
            aqb = arK.alloc([384], BF16)
            akb = arK.alloc([384], BF16)
            avb = [arK.alloc([384], BF16), arK.alloc([384], BF16)]
            qkT = arK.alloc([768], BF16)
            PTr = arK.alloc([6, 128], BF16)
            vg = arK.alloc([256], BF16)
            yTt = [arK.alloc([5, 128], BF16), arK.alloc([5, 128], BF16)]
            wsTf = arK.alloc([4, 128], F32)
            tA = arW.alloc([384], F32)
            tB = arW.alloc([384], F32)
            ro = arW.alloc([384], F32)
            zc = arW.alloc([384], F32)
            tA2 = arW.alloc([384], F32)
            tB2 = arW.alloc([384], F32)
            th = arW.alloc([512], F32)
            sg = arW.alloc([384], F32)
            x2 = arW.alloc([512], F32)
            inn = arW.alloc([512], F32)
            gg = arW.alloc([512], F32)
            sq = arW.alloc([384], F32)
            cen = arW.alloc([384], F32)

            P.dma("sync", "c2", wsTf, wsT_d[l], writes=["wsTf"])
            P.dma("sync", "c2", bsh[:], bsT_d[l], writes=["bsh"])
            P.op("vector", "scalar_tensor_tensor", reads=["wsTf", "cmask"], writes=["wsTb"], out=wsTb[:], in0=wsTf,
                 scalar=0.5, in1=bc(cmask[:], [128, 4, 128], [1]), op0=ALU.mult, op1=ALU.mult)
            P.op("vector", "tensor_scalar", reads=["bsh"], writes=["bsh"], out=bsh[:], in0=bsh[:], scalar1=0.5, scalar2=None,
                 op0=ALU.mult)
            for c in range(8):
                P.dma("gpsimd", "win", Win[:, c, :], w_in_d[l, c * 128:(c + 1) * 128, :], writes=["Win"])
            P.op("vector", "memset", writes=["S_f"], ap=S_f[:], constant=0.0)
            P.op("vector", "memset", writes=["S_b"], ap=S_b[:], constant=0.0)

            def rope(eng, src, src_res, t, dst, dst_res, ta, tb, tag):
                z4 = src.rearrange("p (h a f) -> p h a f", h=6, a=2)
                a4 = ta.rearrange("p (h a f) -> p h a f", h=6, a=2)
                b4 = tb.rearrange("p (h a f) -> p h a f", h=6, a=2)
                P.op(eng, "tensor_tensor", reads=[src_res, "cos_t"], writes=["ta" + tag], out=ta.rearrange("p (g f) -> p g f", g=12),
                     in0=src.rearrange("p (g f) -> p g f", g=12), in1=bc(cos_t[:, t, :], [128, 12, 32], [1]), op=ALU.mult)
                P.op(eng, "tensor_tensor", reads=[src_res, "nsin_t"], writes=["tb0" + tag], out=b4[:, :, 0, :], in0=z4[:, :, 1, :],
                     in1=bc(nsin_t[:, t, :], [128, 6, 32], [1]), op=ALU.mult)
                P.op(eng, "tensor_tensor", reads=[src_res, "sin_t"], writes=["tb1" + tag], out=b4[:, :, 1, :], in0=z4[:, :, 0, :],
                     in1=bc(sin_t[:, t, :], [128, 6, 32], [1]), op=ALU.mult)
                P.op(eng, "tensor_tensor", reads=["ta" + tag, "tb0" + tag, "tb1" + tag], writes=[dst_res], out=dst, in0=ta, in1=tb,
                     op=ALU.add)

            def groupnorm_stats(src3, src_res, ng, c0, var_scale, out_scale, sqbuf):
                s1 = stat[:, c0:c0 + ng]
                s2 = stat[:, c0 + 8:c0 + 8 + ng]
                mean = stat[:, c0 + 16:c0 + 16 + ng]
                msq = stat[:, c0 + 24:c0 + 24 + ng]
                var = stat[:, c0 + 32:c0 + 32 + ng]
                rs = stat[:, c0 + 40:c0 + 40 + ng]
                sq3 = sqbuf.rearrange("p (g d) -> p g d", g=ng)
                P.op("vector", "tensor_reduce", reads=[src_res], writes=["gn_s1"], out=s1, in_=src3, axis=AX.X, op=ALU.add)
                P.op("scalar", "activation", reads=[src_res], writes=["gn_sq"], out=sq3, in_=src3, func=AF.Square)
                P.op("vector", "tensor_reduce", reads=["gn_sq"], writes=["gn_s2"], out=s2, in_=sq3, axis=AX.X, op=ALU.add)
                P.op("vector", "tensor_scalar", reads=["gn_s1"], writes=["gn_mean"], out=mean, in0=s1, scalar1=1.0 / 64, scalar2=None,
                     op0=ALU.mult)
                P.op("vector", "tensor_tensor", reads=["gn_mean"], writes=["gn_msq"], out=msq, in0=mean, in1=mean, op=ALU.mult)
                P.op("vector", "scalar_tensor_tensor", reads=["gn_s2", "gn_msq"], writes=["gn_var"], out=var, in0=s2, scalar=1.0 / 64,
                     in1=msq, op0=ALU.mult, op1=ALU.subtract)
                P.op("vector", "tensor_scalar", reads=["gn_var"], writes=["gn_var"], out=var, in0=var, scalar1=var_scale, scalar2=EPS,
                     op0=ALU.mult, op1=ALU.add)
                P.op("gpsimd", "tensor_tensor", reads=["gn_var", "mhalf"], writes=["gn_rs"], out=rs, in0=var,
                     in1=mhalf[:, 0:ng], op=ALU.pow)
                P.op("vector", "tensor_scalar", reads=["gn_rs"], writes=["gn_rs"], out=rs, in0=rs, scalar1=out_scale, scalar2=None,
                     op0=ALU.mult)
                return mean, rs

            segs = [(0, 384), (384, 768), (768, 1152), (1152, 1536), (1536, 1920), (1920, 2304), (2304, 2688), (2688, 3200)]

            P.dma("sync", "x0", xb[0][:], x_src[0:128, :], writes=["xb0"])
            for t in range(ntiles):
                xt = xb[t % 2]
                xres = "xb%d" % (t % 2)
                if t + 1 < NT:
                    P.dma("sync", "x%d" % ((t + 1) % 2), xb[(t + 1) % 2][:], x_src[(t + 1) * 128:(t + 2) * 128, :],
                          writes=["xb%d" % ((t + 1) % 2)])
                hT = hTb[t % 2]
                hres = "hT%d" % (t % 2)
                make_hT(xt[:], xres, 0, hT, hres)
                tcols = slice(t * 128, (t + 1) * 128)
                for gi, (c0, c1) in enumerate(segs[:p1_groups]):
                    w = c1 - c0
                    Z, zres = next_ps()
                    Z = Z[:, 0:w]
                    for k in range(8):
                        P.op("tensor", "matmul", reads=[hres, "Win"], writes=[zres], out=Z, lhsT=hT[:, k, :], rhs=Win[:, k, c0:c1],
                             start=(k == 0), stop=(k == 7))
                    if gi == 0:
                        rope("vector", Z, zres, t, ro, "ro", tA, tB, "r")
                        P.op("vector", "tensor_tensor", reads=["ro", "rtab"], writes=["qb"], out=qb.rearrange("p (h d) -> p h d", h=6),
                             in0=ro.rearrange("p (h d) -> p h d", h=6), in1=bc(XI, [128, 6, 64], [2]), op=ALU.mult)
                    elif gi == 1:
                        rope("vector", Z, zres, t, ro, "ro", tA, tB, "r")
                        P.op("vector", "tensor_tensor", reads=["ro", "rtab"], writes=["kb1"], out=kb1.rearrange("p (h d) -> p h d", h=6),
                             in0=ro.rearrange("p (h d) -> p h d", h=6), in1=bc(KS1, [128, 6, 64], [2]), op=ALU.mult)
                        P.op("vector", "tensor_tensor", reads=["ro", "rtab"], writes=["kb2"], out=kb2.rearrange("p (h d) -> p h d", h=6),
                             in0=ro.rearrange("p (h d) -> p h d", h=6), in1=bc(KS2, [128, 6, 64], [2]), op=ALU.mult)
                    elif gi == 2:
                        P.op("scalar", "activation", reads=[zres], writes=["vb"], out=vb, in_=Z, func=AF.Copy)
                    elif gi == 3:
                        P.op("scalar", "activation", reads=[zres], writes=["th"], out=th[:, 0:384], in_=Z, func=AF.Tanh, scale=0.5)
                        P.op("vector", "scalar_tensor_tensor", reads=["th", zres], writes=["sg"], out=sg, in0=th[:, 0:384], scalar=1.0,
                             in1=Z, op0=ALU.add, op1=ALU.mult)
                    elif gi in (4, 5):
                        dst, dres = (aqb, "aqb") if gi == 4 else (akb, "akb")
                        P.op("scalar", "activation", reads=[zres], writes=["zc"], out=zc, in_=Z, func=AF.Copy)
                        rope("gpsimd", zc, "zc", t, dst, dres, tA2, tB2, "a")
                        for j in range(3):
                            P.op("tensor", "transpose", reads=[dres, "ident"], writes=["T1"], out=T1[:, j * 128:(j + 1) * 128],
                                 in_=dst[:, j * 128:(j + 1) * 128], identity=ident[:])
                        dT = aqT if gi == 4 else akT
                        P.op("scalar", "activation", reads=["T1"], writes=["aT%d" % gi], out=dT[:, :, tcols],
                             in_=T1[:, 0:384].rearrange("p (a b) -> p a b", a=3), func=AF.Copy)
                    elif gi == 6:
                        av_ = avb[t % 2]
                        P.op("scalar", "activation", reads=[zres], writes=["avb%d" % (t % 2)], out=av_, in_=Z, func=AF.Copy)
                        P.dma("sync", "st_v", vs_d[t * 128:(t + 1) * 128, :], av_, reads=["avb%d" % (t % 2)])
                    else:
                        P.op("scalar", "activation", reads=[zres], writes=["x2"], out=x2, in_=Z, func=AF.Square)
                        P.op("gpsimd", "tensor_scalar", reads=["x2"], writes=["inn"], out=inn, in0=x2, scalar1=0.044715, scalar2=1.0,
                             op0=ALU.mult, op1=ALU.add)
                        P.op("vector", "tensor_tensor", reads=["inn", zres], writes=["inn"], out=inn, in0=inn, in1=Z, op=ALU.mult)
                        P.op("scalar", "activation", reads=["inn"], writes=["th"], out=th, in_=inn, func=AF.Tanh,
                             scale=0.7978845608028654)
                        P.op("vector", "scalar_tensor_tensor", reads=["th", zres], writes=["gg"], out=gg, in0=th, scalar=1.0, in1=Z,
                             op0=ALU.add, op1=ALU.mult)
                        gv3 = gg[:, 256:512].rearrange("p (g d) -> p g d", g=4)
                        mean, rs = groupnorm_stats(gv3, "gg", 4, 8, 0.25, 0.5, sq[:, 0:256])
                        cen3 = cen[:, 0:256].rearrange("p (g d) -> p g d", g=4)
                        P.op("vector", "tensor_tensor", reads=["gg", "gn_mean"], writes=["cen"], out=cen3, in0=gv3,
                             in1=bc(mean, [128, 4, 64], [2]), op=ALU.subtract)
                        P.op("vector", "tensor_tensor", reads=["cen", "gn_rs"], writes=["vg"], out=vg.rearrange("p (g d) -> p g d", g=4),
                             in0=cen3, in1=bc(rs, [128, 4, 64], [2]), op=ALU.mult)
                        M, mres = next_ps()
                        for g in range(4):
                            P.op("tensor", "matmul", reads=["wsTb", "vg"], writes=[mres], out=M[:, g * 64:(g + 1) * 64], lhsT=wsTb[:, g, :],
                                 rhs=vg[:, g * 64:(g + 1) * 64], start=True, stop=True)
                        for g in range(4):
                            P.op("vector", "scalar_tensor_tensor", reads=[mres, "bsh", "gg"], writes=["ytile_gm"],
                                 out=ytile[:, 768 + g * 64:768 + (g + 1) * 64], in0=M[:, g * 64:(g + 1) * 64], scalar=bsh[:, g:g + 1],
                                 in1=gg[:, g * 64:(g + 1) * 64], op0=ALU.add, op1=ALU.mult)
                    if gi == 2:
                        for j in range(3):
                            P.op("tensor", "transpose", reads=["qb", "ident"], writes=["T1"], out=T1[:, j * 128:(j + 1) * 128],
                                 in_=qb[:, j * 128:(j + 1) * 128], identity=ident[:])
                        for j in range(3):
                            P.op("tensor", "transpose", reads=["kb1", "ident"], writes=["T1"], out=T1[:, 384 + j * 128:384 + (j + 1) * 128],
                                 in_=kb1[:, j * 128:(j + 1) * 128], identity=ident[:])
                        P.op("scalar", "activation", reads=["T1"], writes=["qkT"], out=qkT, in_=T1[:, 0:768], func=AF.Copy)
                        SC = [next_ps(), next_ps()]
                        for h in range(6):
                            j, a = h // 2, h % 2
                            sc, scres = SC[a]
                            P.op("tensor", "matmul", reads=["qkT"], writes=[scres], out=sc[:, j * 128:(j + 1) * 128],
                                 lhsT=qkT[64 * a:64 * a + 64, 384 + j * 128:384 + (j + 1) * 128],
                                 rhs=qkT[64 * a:64 * a + 64, j * 128:(j + 1) * 128], start=True, stop=True)
                        for i in range(2):
                            sc, scres = SC[i]
                            P.op("vector", "tensor_tensor", reads=[scres, "cmask"], writes=["PTr%d" % i],
                                 out=PTr.rearrange("p (j a) q -> p a j q", a=2)[:, i, :, :],
                                 in0=sc[:, 0:384].rearrange("p (a b) -> p a b", a=3), in1=bc(cmask[:], [128, 3, 128], [1]), op=ALU.mult)
                        Y, yres = next_ps()
                        for h in range(6):
                            j, a = h // 2, h % 2
                            P.op("tensor", "matmul", reads=["PTr%d" % a, "vb"], writes=[yres], out=Y[:, h * 64:(h + 1) * 64],
                                 lhsT=PTr[:, h, :], rhs=vb[:, h * 64:(h + 1) * 64], start=True, stop=False)
                            P.op("tensor", "matmul", reads=["qkT", "S_b"], writes=[yres], out=Y[:, h * 64:(h + 1) * 64],
                                 lhsT=qkT[64 * a:64 * a + 64, j * 128:(j + 1) * 128], rhs=S_b[64 * a:64 * a + 64, j, :],
                                 start=False, stop=True)
                        INC, ires = next_ps()
                        for h in range(6):
                            j, a = h // 2, h % 2
                            P.op("tensor", "matmul", reads=["kb2", "vb"], writes=[ires], out=INC[64 * a:64 * a + 64, j * 64:(j + 1) * 64],
                                 lhsT=kb2[:, h * 64:(h + 1) * 64], rhs=vb[:, h * 64:(h + 1) * 64], start=True, stop=True)
                        P.op("vector", "tensor_tensor", reads=["S_f", "rtab"], writes=["S_f"], out=S_f[:], in0=S_f[:],
                             in1=bc(DEC, [128, 3, 64], [2]), op=ALU.mult)
                        P.op("vector", "tensor_tensor", reads=["S_f", ires], writes=["S_f"], out=S_f[:], in0=S_f[:],
                             in1=INC[:, 0:192].rearrange("p (a b) -> p a b", a=3), op=ALU.add)
                        P.op("scalar", "activation", reads=["S_f"], writes=["S_b"], out=S_b[:], in_=S_f[:], func=AF.Copy)
                        ret_Y = (Y, yres)
                    if gi == 3:
                        Y, yres = ret_Y
                        Y3 = Y[:, 0:384].rearrange("p (g d) -> p g d", g=6)
                        mean, rs = groupnorm_stats(Y3, yres, 6, 8, 1.0, 0.5, sq)
                        cen3 = cen.rearrange("p (g d) -> p g d", g=6)
                        P.op("vector", "tensor_tensor", reads=[yres, "gn_mean"], writes=["cen"], out=cen3, in0=Y3,
                             in1=bc(mean, [128, 6, 64], [2]), op=ALU.subtract)
                        P.op("vector", "tensor_tensor", reads=["cen", "gn_rs"], writes=["cen"], out=cen3, in0=cen3,
                             in1=bc(rs, [128, 6, 64], [2]), op=ALU.mult)
                        P.op("vector", "tensor_tensor", reads=["cen", "sg"], writes=["ytile_ret"], out=ytile[:, 0:384], in0=cen, in1=sg,
                             op=ALU.mult)
                if p1_groups < 8:
                    continue
                for j in range(3):
                    P.op("tensor", "transpose", reads=["ytile_ret", "ident"], writes=["T1"], out=T1[:, j * 128:(j + 1) * 128],
                         in_=ytile[:, j * 128:(j + 1) * 128], identity=ident[:])
                for j in range(2):
                    P.op("tensor", "transpose", reads=["ytile_gm", "ident"], writes=["T1"], out=T1[:, 384 + j * 128:384 + (j + 1) * 128],
                         in_=ytile[:, 768 + j * 128:768 + (j + 1) * 128], identity=ident[:])
                yt_ = yTt[t % 2]
                P.op("scalar", "activation", reads=["T1"], writes=["yTt%d" % (t % 2)], out=yt_,
                     in_=T1[:, 0:640].rearrange("p (a b) -> p a b", a=5), func=AF.Copy)
                P.dma("sync", "st_y", yT_d[0:384, tcols].rearrange("(c p) n -> p c n", p=128), yt_[:, 0:3, :],
                      reads=["yTt%d" % (t % 2)])
                P.dma("sync", "st_y", yT_d[768:1024, tcols].rearrange("(c p) n -> p c n", p=128), yt_[:, 3:5, :],
                      reads=["yTt%d" % (t % 2)])
            P.barrier()
            if stop_after == (l, 1):
                break

            arW.reset()
            arK.reset()
            acc = arW.alloc([3, 2 * 2048], F32)
            assert arW.off <= 25600
            Vt = [arK.alloc([384], BF16) for _ in range(4)]
            PTa = [arK.alloc([2, 128], BF16) for _ in range(4)]
            yat = [arK.alloc([2048], BF16) for _ in range(2)]
            rden = arK.alloc([2048], F32)
            vcnt = 0
            pcnt = 0
            ucnt = 0
            ycnt = 0
            for n in range(2):
                for pi, r in enumerate((1, 4, 16)):
                    bpc = 16 // r
                    for c in range(r):
                        for b in range(bpc * n, bpc * (n + 1)):
                            has_prev = b > 0
                            vrows = vs_d.rearrange("(m r) d -> r m d", r=r)
                            vbuf = {}
                            for kb in ((0, 1) if has_prev else (1,)):
                                bb = b - 1 + kb
                                vi = vcnt % 4
                                vcnt += 1
                                P.dma("sync", "v%d" % vi, Vt[vi], vrows[c, 128 * bb:128 * (bb + 1), :], writes=["Vt%d" % vi])
                                vbuf[kb] = (Vt[vi], "Vt%d" % vi)
                            kbs = (0, 1) if has_prev else (1,)
                            for j in range(3):
                                NPt, npres = [(Fp[2][:], "F2"), (Fp[3][:], "F3")][ucnt % 2]
                                ucnt += 1
                                for a in range(2):
                                    h = 2 * j + a
                                    SPt, spres = [(Fp[0][:], "F0"), (Fp[1][:], "F1")][pcnt % 2]
                                    PT = PTa[pcnt % 4]
                                    ptres = "PTa%d" % (pcnt % 4)
                                    pcnt += 1
                                    SP3 = SPt[:, 0:256].rearrange("p (a b) -> p a b", a=2)
                                    qv = aqT[64 * a:64 * a + 64, j, :].rearrange("p (m r) -> p r m", r=r)[:, c, 128 * b:128 * (b + 1)]
                                    for kb in kbs:
                                        bb = b - 1 + kb
                                        kv = akT[64 * a:64 * a + 64, j, :].rearrange("p (m r) -> p r m", r=r)[:, c, 128 * bb:128 * (bb + 1)]
                                        P.op("tensor", "matmul", reads=["aT4", "aT5"], writes=[spres], out=SP3[:, kb, :], lhsT=kv, rhs=qv,
                                             start=True, stop=True)
                                    k0 = kbs[0]
                                    P.op("scalar", "activation", reads=[spres], writes=[ptres], out=PT[:, k0:2, :], in_=SP3[:, k0:2, :],
                                         func=AF.Exp, scale=0.125)
                                    P.op("gpsimd", "tensor_tensor", reads=[ptres, "amask"], writes=[ptres], out=PT[:, k0:2, :],
                                         in0=PT[:, k0:2, :], in1=amask[:, k0:2, :], op=ALU.mult)
                                    for which in range(2):
                                        for kb in kbs:
                                            if which == 0:
                                                lhsT = vbuf[kb][0][:, h * 64:(h + 1) * 64]
                                                rd = [vbuf[kb][1], ptres]
                                            else:
                                                lhsT = ones[:]
                                                rd = ["ones", ptres]
                                            P.op("tensor", "matmul", reads=rd, writes=[npres],
                                                 out=NPt[64 * a:64 * a + 64, which * 128:(which + 1) * 128], lhsT=lhsT, rhs=PT[:, kb, :],
                                                 start=(kb == kbs[0]), stop=(kb == kbs[-1]))
                                av = acc[:, j, :].rearrange("p (w m r) -> p w r m", w=2, r=r)[:, :, c,
                                                                                             128 * (b - bpc * n):128 * (b - bpc * n + 1)]
                                np3 = NPt[:, 0:256].rearrange("p (w q) -> p w q", w=2)
                                if pi == 0:
                                    P.op("vector", "tensor_copy", reads=[npres], writes=["acc%d" % j], out=av, in_=np3)
                                else:
                                    P.op("vector", "tensor_tensor", reads=[npres, "acc%d" % j], writes=["acc%d" % j], out=av, in0=np3, in1=av,
                                         op=ALU.add)
                for j in range(3):
                    yb = yat[ycnt % 2]
                    yres = "yat%d" % (ycnt % 2)
                    ycnt += 1
                    P.op("vector", "reciprocal", reads=["acc%d" % j], writes=["rden"], out=rden, in_=acc[:, j, 2048:4096])
                    P.op("gpsimd", "tensor_tensor", reads=["acc%d" % j, "rden"], writes=[yres], out=yb, in0=acc[:, j, 0:2048], in1=rden,
                         op=ALU.mult)
                    P.dma("sync", "st_y", yT_d[384 + 128 * j:384 + 128 * (j + 1), 2048 * n:2048 * (n + 1)], yb, reads=[yres])
            P.barrier()
            if stop_after == (l, 2):
                break

            arW.reset()
            arK.reset()
            Wout = arW.alloc([8, D], BF16)
            ptmp = arK.alloc([D], F32)
            yTl = [arK.alloc([8, 128], BF16), arK.alloc([8, 128], BF16)]
            for c in range(8):
                P.dma("gpsimd", "wout", Wout[:, c, :], w_out_d[l, c * 128:(c + 1) * 128, :], writes=["Wout"])
            P.dma("sync", "c3", gpost[:], gpost_in[l, 0], writes=["gpost"])
            x_dst = xs_d

            def p3_load(t):
                P.dma("sync", "x%d" % (t % 2), xb[t % 2][:], x_src[t * 128:(t + 1) * 128, :], writes=["xb%d" % (t % 2)])
                P.dma("sync", "yl%d" % (t % 2), yTl[t % 2], yT_d[:, t * 128:(t + 1) * 128].rearrange("(c p) n -> p c n", p=128),
                      writes=["yTl%d" % (t % 2)])
            p3_load(0)
            for t in range(NT):
                if t + 1 < NT:
                    p3_load(t + 1)
                xt = xb[t % 2]
                xres = "xb%d" % (t % 2)
                for hf in range(2):
                    for c in range(8):
                        P.op("tensor", "matmul", reads=["yTl%d" % (t % 2), "Wout"], writes=["Dp"], out=Dp[:, hf * 512:(hf + 1) * 512],
                             lhsT=yTl[t % 2][:, c, :], rhs=Wout[:, c, hf * 512:(hf + 1) * 512], start=(c == 0), stop=(c == 7))
                post_norm_residual(xt[:], xres)
                P.dma("sync", "st_x", x_dst[t * 128:(t + 1) * 128, :], xt[:], reads=[xres])
            x_src = xs_d
            P.barrier()
            if stop_after == (l, 3):
                break

            arW.reset()
            arK.reset()
            Wgu = arW.alloc([8, 2 * FH], BF16)
            Wdn = arW.alloc([22, D], BF16)
            xn = arK.alloc([D], BF16)
            ptmp = arK.alloc([D], F32)
            hTb = [arK.alloc([8, 128], BF16), arK.alloc([8, 128], BF16)]
            act = arK.alloc([FH], BF16)
            actT = arK.alloc([22, 128], BF16)
            thb = [arK.alloc([512], F32), arK.alloc([512], F32)]
            ub = [arK.alloc([512], F32), arK.alloc([512], F32)]
            for c in range(8):
                P.dma("gpsimd", "wgu", Wgu[:, c, :], w_gu_d[l, c * 128:(c + 1) * 128, :], writes=["Wgu"])
            for i in range(2):
                P.dma("gpsimd", "wdn", Wdn[:, 11 * i:11 * (i + 1), :],
                      w_dn_d[l, 11 * 128 * i:11 * 128 * (i + 1), :].rearrange("(c p) n -> p c n", p=128), writes=["Wdn"])
            P.dma("sync", "c3", gpost[:], gpost_in[l, 1], writes=["gpost"])
            x_dst = out_d if l == n_layers - 1 else xs_d
            pieces = [(i * 512, 512) for i in range(5)] + [(2560, 256)]
            P.dma("sync", "x0", xb[0][:], x_src[0:128, :], writes=["xb0"])
            pc = 0
            for t in range(NT):
                if t + 1 < NT:
                    P.dma("sync", "x%d" % ((t + 1) % 2), xb[(t + 1) % 2][:], x_src[(t + 1) * 128:(t + 2) * 128, :],
                          writes=["xb%d" % ((t + 1) % 2)])
                xt = xb[t % 2]
                xres = "xb%d" % (t % 2)
                hT = hTb[t % 2]
                hres = "hT%d" % (t % 2)
                make_hT(xt[:], xres, 1, hT, hres)
                for (p0, pw) in pieces:
                    G, gres = (Fp[0][:], "F0") if pc % 2 == 0 else (Fp[2][:], "F2")
                    U, ures = (Fp[1][:], "F1") if pc % 2 == 0 else (Fp[3][:], "F3")
                    th_, thres = thb[pc % 2], "thb%d" % (pc % 2)
                    u_, ures2 = ub[pc % 2], "ub%d" % (pc % 2)
                    pc += 1
                    for k in range(8):
                        P.op("tensor", "matmul", reads=[hres, "Wgu"], writes=[gres], out=G[:, 0:pw], lhsT=hT[:, k, :],
                             rhs=Wgu[:, k, p0:p0 + pw], start=(k == 0), stop=(k == 7))
                    for k in range(8):
                        P.op("tensor", "matmul", reads=[hres, "Wgu"], writes=[ures], out=U[:, 0:pw], lhsT=hT[:, k, :],
                             rhs=Wgu[:, k, FH + p0:FH + p0 + pw], start=(k == 0), stop=(k == 7))
                    P.op("scalar", "activation", reads=[gres], writes=[thres], out=th_[:, 0:pw], in_=G[:, 0:pw], func=AF.Tanh, scale=0.5)
                    P.op("vector", "scalar_tensor_tensor", reads=[thres, gres], writes=[ures2], out=u_[:, 0:pw], in0=th_[:, 0:pw],
                         scalar=1.0, in1=G[:, 0:pw], op0=ALU.add, op1=ALU.mult)
                    P.op("vector", "scalar_tensor_tensor", reads=[ures2, ures], writes=["act"], out=act[:, p0:p0 + pw], in0=u_[:, 0:pw],
                         scalar=0.5, in1=U[:, 0:pw], op0=ALU.mult, op1=ALU.mult)
                for rnd in range(3):
                    cs = list(range(rnd * 8, min(22, rnd * 8 + 8)))
                    for i, c in enumerate(cs):
                        P.op("tensor", "transpose", reads=["act", "ident"], writes=["T1"], out=T1[:, i * 128:(i + 1) * 128],
                             in_=act[:, c * 128:(c + 1) * 128], identity=ident[:])
                    P.op("scalar", "activation", reads=["T1"], writes=["actT"], out=actT[:, cs[0]:cs[-1] + 1, :],
                         in_=T1[:, 0:128 * len(cs)].rearrange("p (a b) -> p a b", a=len(cs)), func=AF.Copy)
                for hf in range(2):
                    for c in range(22):
                        P.op("tensor", "matmul", reads=["actT", "Wdn"], writes=["Dp"], out=Dp[:, hf * 512:(hf + 1) * 512],
                             lhsT=actT[:, c, :], rhs=Wdn[:, c, hf * 512:(hf + 1) * 512], start=(c == 0), stop=(c == 21))
                post_norm_residual(xt[:], xres)
                P.dma("sync", "st_x", x_dst[t * 128:(t + 1) * 128, :], xt[:], reads=[xres])
            x_src = xs_d
            P.barrier()
        P.emit()
    return nc


def _consts():
    ident = np.eye(128, dtype=np.float32).astype(ml_dtypes.bfloat16)
    k = np.arange(128)[:, None]
    q = np.arange(128)[None, :]
    prev = (k >= q).astype(np.float32)
    cur = (k <= q).astype(np.float32)
    amask = np.concatenate([prev, cur], axis=1).astype(ml_dtypes.bfloat16)
    cmask = cur.astype(ml_dtypes.bfloat16)
    inv_freq = (10000.0 ** (-np.arange(0, 64, 2, dtype=np.float32) / np.float32(64))).astype(np.float32)
    invf = np.broadcast_to(inv_freq[None, :], (128, 32)).copy()
    h = np.arange(6, dtype=np.float64)
    lg = np.log(1.0 - 2.0 ** (-5.0 - h))
    p = np.arange(128, dtype=np.float64)[:, None]
    xi = np.exp(lg[None, :] * (p + 1.0))
    ks1 = np.exp(-lg[None, :] * (p + 1.0)) / 8.0
    ks2 = np.exp(lg[None, :] * (127.0 - p)) / 8.0
    dec = np.zeros((128, 3))
    for j in range(3):
        dec[0:64, j] = np.exp(lg[2 * j] * 128.0)
        dec[64:128, j] = np.exp(lg[2 * j + 1] * 128.0)
    rtab = np.concatenate([xi, ks1, ks2, dec], axis=1).astype(np.float32)
    return dict(ident=ident, amask=amask, cmask=cmask, invf=invf, rtab=rtab)


def make_in_maps(x, positions, g_mix_pre, g_mix_post, w_in, w_s, b_s, w_out, g_ffn_pre, g_ffn_post, w_gate_up, w_down):
    f32 = np.float32
    c = _consts()
    gpre = np.stack([np.asarray(g_mix_pre, f32), np.asarray(g_ffn_pre, f32)], axis=1)
    gpre = np.ascontiguousarray(gpre.reshape(L, 2, 8, 128).transpose(0, 1, 3, 2))
    gpost = np.stack([np.asarray(g_mix_post, f32), np.asarray(g_ffn_post, f32)], axis=1)
    gpost = np.ascontiguousarray(np.broadcast_to(gpost[:, :, None, :], (L, 2, 128, D)))
    wsT = np.ascontiguousarray(np.asarray(w_s, f32).transpose(0, 3, 1, 2))
    bsT = np.ascontiguousarray(np.asarray(b_s, f32).transpose(0, 2, 1))
    shared = dict(gpre=gpre, gpost=gpost, w_in=np.ascontiguousarray(w_in, dtype=f32), wsT=wsT, bsT=bsT,
                  w_out=np.ascontiguousarray(w_out, dtype=f32), w_gu=np.ascontiguousarray(w_gate_up, dtype=f32),
                  w_dn=np.ascontiguousarray(w_down, dtype=f32), **c)
    maps = []
    for b in range(x.shape[0]):
        pos = np.ascontiguousarray(np.asarray(positions[b], np.int32).reshape(NT, 128).T)
        m = dict(shared)
        m["x"] = np.ascontiguousarray(x[b], dtype=f32)
        m["pos"] = pos
        maps.append(m)
    return maps


_NC_CACHE = {}


def kernel(x, positions, g_mix_pre, g_mix_post, w_in, w_s, b_s, w_out, g_ffn_pre, g_ffn_post, w_gate_up, w_down):
    x = np.asarray(x)
    B = x.shape[0]
    if "nc" not in _NC_CACHE:
        _NC_CACHE["nc"] = build_nc()
    nc = _NC_CACHE["nc"]
    maps = make_in_maps(x, np.asarray(positions), g_mix_pre, g_mix_post, w_in, w_s, b_s, w_out, g_ffn_pre, g_ffn_post,
                        w_gate_up, w_down)
    res = run_bass_kernel_spmd(nc, maps, core_ids=list(range(B)))
    out = np.stack([np.asarray(r["out"], dtype=np.float32) for r in res.results], axis=0)
    return out
```

```python
import contextlib
import math
import numpy as np
import ml_dtypes
import concourse.bass as bass
import concourse.mybir as mybir
from concourse.bass_utils import run_bass_kernel_spmd

F32 = mybir.dt.float32
BF16 = mybir.dt.bfloat16
I32 = mybir.dt.int32
AF = mybir.ActivationFunctionType
ALU = mybir.AluOpType
AX = mybir.AxisListType

D = 1024
S = 4096
NT = 32
L = 2
EPS = 1e-6
FH = 2816
ENGS = ("sync", "scalar", "vector", "gpsimd", "tensor")


class Prog:
    def __init__(self, nc):
        self.nc = nc
        self.ops = []
        self.last_w = {}
        self.readers = {}
        self.dma_cnt = {}
        self.dma_seen = {}
        self.last_on_eng = {}
        self.bar_ops = set()
        self.bar_dma = {}
        self.bar_pending = set()

    def add(self, eng, fn, reads=(), writes=(), dma=None):
        deps = set()
        for r in reads:
            if r in self.last_w:
                deps.add(self.last_w[r])
        for w in writes:
            if w in self.last_w:
                deps.add(self.last_w[w])
            for x in self.readers.get(w, ()):
                deps.add(x)
        dmadeps = {}
        if eng in self.bar_pending:
            self.bar_pending.discard(eng)
            deps |= self.bar_ops
            for k, v in self.bar_dma.items():
                dmadeps[k] = v
        idx = len(self.ops)
        op = dict(eng=eng, fn=fn, deps=deps, dma=dma, dmawait=0, sig=False, dmadeps=dmadeps)
        for d in list(deps):
            dop = self.ops[d]
            if dop["dma"] is not None:
                k = dop["dma"]
                if dma is not None and k == dma:
                    deps.discard(d)
                    continue
                self.dma_seen[k] = True
                dmadeps[k] = max(dmadeps.get(k, 0), self.dma_cnt[k] * 16)
            elif dop["eng"] == "tensor" and eng == "tensor" and dma is None:
                deps.discard(d)
            else:
                dop["sig"] = True
        if dma is not None:
            c = self.dma_cnt.get(dma, 0)
            if self.dma_seen.get(dma, False):
                op["dmawait"] = c * 16
            self.dma_cnt[dma] = c + 1
            self.dma_seen[dma] = False
        self.ops.append(op)
        if dma is None:
            self.last_on_eng[eng] = idx
        for r in reads:
            self.readers.setdefault(r, []).append(idx)
        for w in writes:
            self.last_w[w] = idx
            self.readers[w] = []
        return idx

    def op(self, eng, name, reads=(), writes=(), **kw):
        return self.add(eng, lambda e: getattr(e, name)(**kw), reads, writes)

    def dma(self, eng, key, out, in_, reads=(), writes=()):
        return self.add(eng, lambda e: e.dma_start(out=out, in_=in_), reads, writes, dma=key)

    def barrier(self):
        self.bar_ops = set(self.last_on_eng.values())
        self.bar_dma = {k: c * 16 for k, c in self.dma_cnt.items()}
        for k in self.dma_cnt:
            self.dma_seen[k] = True
        self.bar_pending = set(ENGS)

    def emit(self):
        nc = self.nc
        cnt = {e: 0 for e in ENGS}
        for op in self.ops:
            if op["dma"] is None and op["sig"]:
                cnt[op["eng"]] += 1
                op["sigval"] = cnt[op["eng"]]
        streams = {e: [] for e in ENGS}
        for i, op in enumerate(self.ops):
            streams[op["eng"]].append(i)
        dmakeys = sorted(self.dma_cnt.keys())
        ops = self.ops
        with contextlib.ExitStack() as st:
            esem = {e: st.enter_context(nc.semaphore("s_" + e)) for e in ENGS}
            dsem = {k: st.enter_context(nc.semaphore("d_" + k)) for k in dmakeys}
            block = st.enter_context(nc.Block())

            def run(ename):
                def body(eng):
                    waited = {}

                    def wait(sem, key, val):
                        if val > waited.get(key, 0):
                            eng.wait_ge(sem, val)
                            waited[key] = val

                    for i in streams[ename]:
                        op = ops[i]
                        for d in sorted(op["deps"]):
                            dop = ops[d]
                            if dop["dma"] is None:
                                wait(esem[dop["eng"]], "e" + dop["eng"], dop["sigval"])
                        for k, v in op["dmadeps"].items():
                            wait(dsem[k], "d" + k, v)
                        if op["dma"] is not None and op["dmawait"]:
                            wait(dsem[op["dma"]], "d" + op["dma"], op["dmawait"])
                        ins = op["fn"](eng)
                        if op["dma"] is not None:
                            ins.then_inc(dsem[op["dma"]], 16)
                        elif op["sig"]:
                            ins.then_inc(esem[ename], 1)
                    if ename == "sync":
                        for k in dmakeys:
                            wait(dsem[k], "d" + k, self.dma_cnt[k] * 16)
                return body

            block.sync(run("sync"))
            block.scalar(run("scalar"))
            block.vector(run("vector"))
            block.gpsimd(run("gpsimd"))
            block.tensor(run("tensor"))


class Arena:
    def __init__(self, ap, n):
        self.ap = ap
        self.n = n
        self.off = 0

    def alloc(self, free_shape, dt):
        nel = int(np.prod(free_shape))
        n16 = nel * 2 if dt == F32 else nel
        if self.off % 2:
            self.off += 1
        assert self.off + n16 <= self.n, ("arena overflow", self.off, n16, self.n)
        sl = self.ap[:, self.off:self.off + n16]
        self.off += n16
        if dt == F32:
            sl = sl.bitcast(F32)
        if len(free_shape) == 2:
            sl = sl.rearrange("p (a b) -> p a b", a=free_shape[0])
        elif len(free_shape) == 3:
            sl = sl.rearrange("p (a b c) -> p a b c", a=free_shape[0], b=free_shape[1])
        return sl

    def reset(self, off=0):
        self.off = off


def bc(ap, shape, axes):
    for a in axes:
        ap = ap.unsqueeze(a)
    return ap.to_broadcast(list(shape))


def build_nc(n_layers=L, stop_after=None, debug=False, ntiles=NT, p1_groups=8, p4_tiles=NT):
    nc = bass.Bass("TRN2", target_bir_lowering=False)
    dt_in = lambda name, shape, dt: nc.dram_tensor(name, list(shape), dt, kind="ExternalInput").ap()
    sbn = lambda n: "sb_" + n
    x_in = dt_in("x", [S, D], F32)
    pos_in = dt_in("pos", [128, NT], I32)
    gpre_in = dt_in("gpre", [L, 2, 128, 8], F32)
    gpost_in = dt_in("gpost", [L, 2, 128, D], F32)
    w_in_d = dt_in("w_in", [L, D, 3200], F32)
    wsT_d = dt_in("wsT", [L, 128, 4, 128], F32)
    bsT_d = dt_in("bsT", [L, 128, 4], F32)
    w_out_d = dt_in("w_out", [L, D, D], F32)
    w_gu_d = dt_in("w_gu", [L, D, 2 * FH], F32)
    w_dn_d = dt_in("w_dn", [L, FH, D], F32)
    ident_d = dt_in("ident", [128, 128], BF16)
    amask_d = dt_in("amask", [128, 256], BF16)
    cmask_d = dt_in("cmask", [128, 128], BF16)
    invf_d = dt_in("invf", [128, 32], F32)
    rtab_d = dt_in("rtab", [128, 21], F32)
    out_d = nc.dram_tensor("out", [S, D], F32, kind="ExternalOutput").ap()
    skind = "ExternalOutput" if debug else "Internal"
    xs_d = nc.dram_tensor("xs", [S, D], F32, kind=skind).ap()
    vs_d = nc.dram_tensor("vs", [S, 384], BF16, kind=skind).ap()
    yT_d = nc.dram_tensor("yTs", [D, S], BF16, kind=skind).ap()

    WN = 67584
    KN = 15360
    with contextlib.ExitStack() as st:
        sbt = lambda name, shape, dt: st.enter_context(nc.sbuf_tensor("sb_" + name, list(shape), dt))
        pst = lambda name, shape, dt: st.enter_context(nc.psum_tensor(name, list(shape), dt))
        arW_t = sbt("arW", [128, WN], BF16)
        arK_t = sbt("arK", [128, KN], BF16)
        ident = sbt("ident", [128, 128], BF16)
        amask = sbt("amask", [128, 2, 128], BF16)
        cmask = sbt("cmask", [128, 128], BF16)
        ones = sbt("ones", [128, 64], BF16)
        invf = sbt("invf", [128, 32], F32)
        rtab = sbt("rtab", [128, 21], F32)
        cos_t = sbt("cos_t", [128, NT, 32], F32)
        sin_t = sbt("sin_t", [128, NT, 32], F32)
        nsin_t = sbt("nsin_t", [128, NT, 32], F32)
        pos_i = sbt("pos_i", [128, NT], I32)
        gpre = sbt("gpre", [128, 2, 8], F32)
        gpost = sbt("gpost", [128, D], F32)
        wsTb = sbt("wsTb", [128, 4, 128], BF16)
        bsh = sbt("bsh", [128, 4], F32)
        mhalf = sbt("mhalf", [128, 8], F32)
        stat = sbt("stat", [128, 64], F32)
        S_f = sbt("S_f", [128, 3, 64], F32)
        S_b = sbt("S_b", [128, 3, 64], BF16)
        xb = [sbt("xb0", [128, D], F32), sbt("xb1", [128, D], F32)]

        T0 = pst("T0", [128, 1024], BF16)
        T1 = pst("T1", [128, 1024], BF16)
        Dp = pst("Dp", [128, 1024], F32)
        Fp = [pst("F%d" % i, [128, 512], F32) for i in range(4)]

        arW = Arena(arW_t[:], WN)
        arK = Arena(arK_t[:], KN)
        P = Prog(nc)
        pool = [(Fp[0][:], "F0"), (Fp[1][:], "F1"), (Fp[2][:], "F2"), (Fp[3][:], "F3"),
                (Dp[:, 0:512], "Dlo"), (Dp[:, 512:1024], "Dhi")]
        pool_i = [0]

        def next_ps():
            r = pool[pool_i[0] % len(pool)]
            pool_i[0] += 1
            return r

        XI = rtab[:, 0:6]
        KS1 = rtab[:, 6:12]
        KS2 = rtab[:, 12:18]
        DEC = rtab[:, 18:21]

        P.dma("sync", "c0", ident[:], ident_d, writes=["ident"])
        P.dma("sync", "c0", amask[:], amask_d.rearrange("p (a b) -> p a b", a=2), writes=["amask"])
        P.dma("sync", "c0", cmask[:], cmask_d, writes=["cmask"])
        P.dma("sync", "c0", invf[:], invf_d, writes=["invf"])
        P.dma("sync", "c0", rtab[:], rtab_d, writes=["rtab"])
        P.dma("sync", "c0", pos_i[:], pos_in, writes=["pos_i"])
        P.op("gpsimd", "memset", writes=["mhalf"], ap=mhalf[:], constant=-0.5)
        P.op("gpsimd", "memset", writes=["ones"], ap=ones[:], constant=1.0)
        arK.reset()
        pos_f = arK.alloc([NT], F32)
        ang = arK.alloc([NT, 32], F32)
        kf = arK.alloc([NT, 32], F32)
        rr = arK.alloc([NT, 32], F32)
        mm = arK.alloc([NT, 32], F32)
        k_i3 = arK.alloc([NT, 32], F32).bitcast(I32)
        TWO_PI = 2.0 * math.pi
        C1 = 6.28125
        C2 = TWO_PI - C1
        PI_LO = 3.1415925
        P.op("vector", "tensor_copy", reads=["pos_i"], writes=["pos_f"], out=pos_f, in_=pos_i[:])
        P.op("vector", "tensor_tensor", reads=["pos_f", "invf"], writes=["ang"], out=ang,
             in0=bc(pos_f, [128, NT, 32], [2]), in1=bc(invf[:], [128, NT, 32], [1]), op=ALU.mult)
        P.op("vector", "tensor_scalar", reads=["ang"], writes=["k_i"], out=k_i3, in0=ang, scalar1=1.0 / TWO_PI,
             scalar2=None, op0=ALU.mult)
        P.op("vector", "tensor_copy", reads=["k_i"], writes=["kf"], out=kf, in_=k_i3)
        P.op("vector", "scalar_tensor_tensor", reads=["kf", "ang"], writes=["rr"], out=rr, in0=kf, scalar=-C1, in1=ang,
             op0=ALU.mult, op1=ALU.add)
        P.op("vector", "scalar_tensor_tensor", reads=["kf", "rr"], writes=["rr"], out=rr, in0=kf, scalar=-C2, in1=rr,
             op0=ALU.mult, op1=ALU.add)
        P.op("vector", "tensor_scalar", reads=["rr"], writes=["rr"], out=rr, in0=rr, scalar1=PI_LO, scalar2=-PI_LO,
             op0=ALU.min, op1=ALU.max)
        P.op("scalar", "activation", reads=["rr"], writes=["sin_t"], out=sin_t[:], in_=rr, func=AF.Sin)
        P.op("vector", "tensor_scalar", reads=["sin_t"], writes=["nsin_t"], out=nsin_t[:], in0=sin_t[:], scalar1=-1.0,
             scalar2=None, op0=ALU.mult)
        P.op("vector", "tensor_scalar", reads=["rr"], writes=["kf"], out=kf, in0=rr, scalar1=0.5 * math.pi, scalar2=None,
             op0=ALU.add)
        P.op("vector", "tensor_scalar", reads=["kf"], writes=["mm"], out=mm, in0=kf, scalar1=math.pi, scalar2=None,
             op0=ALU.is_gt)
        P.op("vector", "scalar_tensor_tensor", reads=["mm", "kf"], writes=["kf"], out=kf, in0=mm, scalar=-TWO_PI, in1=kf,
             op0=ALU.mult, op1=ALU.add)
        P.op("vector", "tensor_scalar", reads=["kf"], writes=["kf"], out=kf, in0=kf, scalar1=PI_LO, scalar2=-PI_LO,
             op0=ALU.min, op1=ALU.max)
        P.op("scalar", "activation", reads=["kf"], writes=["cos_t"], out=cos_t[:], in_=kf, func=AF.Sin)
        P.barrier()
        if stop_after == (0, 0):
            P.dma("sync", "dbg", xs_d[0:128, 0:32], cos_t[:, 0, :], reads=["cos_t"])
            P.dma("sync", "dbg", xs_d[0:128, 32:64], sin_t[:, 0, :], reads=["sin_t"])
            P.dma("sync", "dbg", xs_d[128:256, 0:32], cos_t[:, 31, :], reads=["cos_t"])
            P.dma("sync", "dbg", xs_d[128:256, 32:64], sin_t[:, 31, :], reads=["sin_t"])
            n_layers = 0

        def rms_rstd(src, src_res, col, junk, junk_res):
            P.op("scalar", "activation", reads=[src_res], writes=[junk_res, "st%d" % col], out=junk, in_=src,
                 func=AF.Square, accum_out=stat[:, col:col + 1])
            P.op("vector", "tensor_scalar", reads=["st%d" % col], writes=["st%d" % (col + 1)],
                 out=stat[:, col + 1:col + 2], in0=stat[:, col:col + 1], scalar1=1.0 / D, scalar2=EPS,
                 op0=ALU.mult, op1=ALU.add)
            P.op("gpsimd", "tensor_tensor", reads=["st%d" % (col + 1), "mhalf"], writes=["st%d" % (col + 2)],
                 out=stat[:, col + 2:col + 3], in0=stat[:, col + 1:col + 2], in1=mhalf[:, 0:1], op=ALU.pow)
            return stat[:, col + 2:col + 3], "st%d" % (col + 2)

        def make_xn(xt, xres):
            rstd, rres = rms_rstd(xt, xres, 0, xn, "xn")
            P.op("vector", "tensor_scalar", reads=[xres, rres], writes=["xn"], out=xn, in0=xt, scalar1=rstd, scalar2=None,
                 op0=ALU.mult)

        def make_hT2(gsel, hT, hres):
            for c in range(8):
                P.op("tensor", "transpose", reads=["xn", "ident"], writes=["T0"], out=T0[:, c * 128:(c + 1) * 128],
                     in_=xn[:, c * 128:(c + 1) * 128], identity=ident[:])
            P.op("vector", "tensor_tensor", reads=["T0", "gpre"], writes=[hres], out=hT,
                 in0=T0[:].rearrange("p (a b) -> p a b", a=8), in1=bc(gpre[:, gsel, :], [128, 8, 128], [2]), op=ALU.mult)

        def make_hT(xt, xres, gsel, hT, hres):
            make_xn(xt, xres)
            make_hT2(gsel, hT, hres)

        def post_norm_residual(xt, xres):
            rstd, rres = rms_rstd(Dp[:], "Dp", 4, ptmp.bitcast(BF16)[:, 0:D], "ptmp")
            P.op("vector", "scalar_tensor_tensor", reads=["Dp", rres, "gpost"], writes=["ptmp"], out=ptmp, in0=Dp[:],
                 scalar=rstd, in1=gpost[:], op0=ALU.mult, op1=ALU.mult)
            P.op("gpsimd", "tensor_tensor", reads=["ptmp", xres], writes=[xres], out=xt, in0=xt, in1=ptmp, op=ALU.add)

        def load_layer_consts(l):
            P.dma("sync", "c1", gpre[:, 0, :], gpre_in[l, 0], writes=["gpre"])
            P.dma("sync", "c1", gpre[:, 1, :], gpre_in[l, 1], writes=["gpre"])

        x_src = x_in
        for l in range(n_layers):
            load_layer_consts(l)
            arW.reset()
            arK.reset()
            Win = arW.alloc([8, 3200], BF16)
            aqT = arW.alloc([3, S], BF16)
            akT = arW.alloc([3, S], BF16)
            xn = arK.alloc([D], BF16)
            hTb = [arK.alloc([8, 128], BF16), arK.alloc([8, 128], BF16)]
            ytile = arK.alloc([D], BF16)
            qb = arK.alloc([384], BF16)
            kb1 = arK.alloc([384], BF16)
            kb2 = arK.alloc([384], BF16)
            vb = arK.alloc([384], BF16)
            aqb = arK.alloc([384], BF16)
            akb = arK.alloc([384], BF16)
            avb = [arK.alloc([384], BF16), arK.alloc([384], BF16)]
            qkT = arK.alloc([768], BF16)
            PTr = arK.alloc([6, 128], BF16)
            vg = arK.alloc([256], BF16)
            yTt = [arK.alloc([5, 128], BF16), arK.alloc([5, 128], BF16)]
            wsTf = arK.alloc([4, 128], F32)
            tA = arW.alloc([384], F32)
            tB = arW.alloc([384], F32)
            ro = arW.alloc([384], F32)
            zc = arW.alloc([384], F32)
            tA2 = arW.alloc([384], F32)
            tB2 = arW.alloc([384], F32)
            th = arW.alloc([512], F32)
            sg = arW.alloc([384], F32)
            x2 = arW.alloc([512], F32)
            inn = arW.alloc([512], F32)
            gg = arW.alloc([512], F32)
            sq = arW.alloc([384], F32)
            cen = arW.alloc([384], F32)

            P.dma("sync", "c2", wsTf, wsT_d[l], writes=["wsTf"])
            P.dma("sync", "c2", bsh[:], bsT_d[l], writes=["bsh"])
            P.op("vector", "scalar_tensor_tensor", reads=["wsTf", "cmask"], writes=["wsTb"], out=wsTb[:], in0=wsTf,
                 scalar=0.5, in1=bc(cmask[:], [128, 4, 128], [1]), op0=ALU.mult, op1=ALU.mult)
            P.op("vector", "tensor_scalar", reads=["bsh"], writes=["bsh"], out=bsh[:], in0=bsh[:], scalar1=0.5, scalar2=None,
                 op0=ALU.mult)
            for c in range(8):
                P.dma("gpsimd", "win", Win[:, c, :], w_in_d[l, c * 128:(c + 1) * 128, :], writes=["Win"])
            P.op("vector", "memset", writes=["S_f"], ap=S_f[:], constant=0.0)
            P.op("vector", "memset", writes=["S_b"], ap=S_b[:], constant=0.0)

            def rope(eng, src, src_res, t, dst, dst_res, ta, tb, tag):
                z4 = src.rearrange("p (h a f) -> p h a f", h=6, a=2)
                a4 = ta.rearrange("p (h a f) -> p h a f", h=6, a=2)
                b4 = tb.rearrange("p (h a f) -> p h a f", h=6, a=2)
                P.op(eng, "tensor_tensor", reads=[src_res, "cos_t"], writes=["ta" + tag], out=ta.rearrange("p (g f) -> p g f", g=12),
                     in0=src.rearrange("p (g f) -> p g f", g=12), in1=bc(cos_t[:, t, :], [128, 12, 32], [1]), op=ALU.mult)
                P.op(eng, "tensor_tensor", reads=[src_res, "nsin_t"], writes=["tb0" + tag], out=b4[:, :, 0, :], in0=z4[:, :, 1, :],
                     in1=bc(nsin_t[:, t, :], [128, 6, 32], [1]), op=ALU.mult)
                P.op(eng, "tensor_tensor", reads=[src_res, "sin_t"], writes=["tb1" + tag], out=b4[:, :, 1, :], in0=z4[:, :, 0, :],
                     in1=bc(sin_t[:, t, :], [128, 6, 32], [1]), op=ALU.mult)
                P.op(eng, "tensor_tensor", reads=["ta" + tag, "tb0" + tag, "tb1" + tag], writes=[dst_res], out=dst, in0=ta, in1=tb,
                     op=ALU.add)

            def groupnorm_stats(src3, src_res, ng, c0, var_scale, out_scale, sqbuf):
                s1 = stat[:, c0:c0 + ng]
                s2 = stat[:, c0 + 8:c0 + 8 + ng]
                mean = stat[:, c0 + 16:c0 + 16 + ng]
                msq = stat[:, c0 + 24:c0 + 24 + ng]
                var = stat[:, c0 + 32:c0 + 32 + ng]
                rs = stat[:, c0 + 40:c0 + 40 + ng]
                sq3 = sqbuf.rearrange("p (g d) -> p g d", g=ng)
                P.op("vector", "tensor_reduce", reads=[src_res], writes=["gn_s1"], out=s1, in_=src3, axis=AX.X, op=ALU.add)
                P.op("scalar", "activation", reads=[src_res], writes=["gn_sq"], out=sq3, in_=src3, func=AF.Square)
                P.op("vector", "tensor_reduce", reads=["gn_sq"], writes=["gn_s2"], out=s2, in_=sq3, axis=AX.X, op=ALU.add)
                P.op("vector", "tensor_scalar", reads=["gn_s1"], writes=["gn_mean"], out=mean, in0=s1, scalar1=1.0 / 64, scalar2=None,
                     op0=ALU.mult)
                P.op("vector", "tensor_tensor", reads=["gn_mean"], writes=["gn_msq"], out=msq, in0=mean, in1=mean, op=ALU.mult)
                P.op("vector", "scalar_tensor_tensor", reads=["gn_s2", "gn_msq"], writes=["gn_var"], out=var, in0=s2, scalar=1.0 / 64,
                     in1=msq, op0=ALU.mult, op1=ALU.subtract)
                P.op("vector", "tensor_scalar", reads=["gn_var"], writes=["gn_var"], out=var, in0=var, scalar1=var_scale, scalar2=EPS,
                     op0=ALU.mult, op1=ALU.add)
                P.op("gpsimd", "tensor_tensor", reads=["gn_var", "mhalf"], writes=["gn_rs"], out=rs, in0=var,
                     in1=mhalf[:, 0:ng], op=ALU.pow)
                P.op("vector", "tensor_scalar", reads=["gn_rs"], writes=["gn_rs"], out=rs, in0=rs, scalar1=out_scale, scalar2=None,
                     op0=ALU.mult)
                return mean, rs

            segs = [(0, 384), (384, 768), (768, 1152), (1152, 1536), (1536, 1920), (1920, 2304), (2304, 2688), (2688, 3200)]

            P.dma("sync", "x0", xb[0][:], x_src[0:128, :], writes=["xb0"])
            for t in range(ntiles):
                xt = xb[t % 2]
                xres = "xb%d" % (t % 2)
                if t + 1 < NT:
                    P.dma("sync", "x%d" % ((t + 1) % 2), xb[(t + 1) % 2][:], x_src[(t + 1) * 128:(t + 2) * 128, :],
                          writes=["xb%d" % ((t + 1) % 2)])
                hT = hTb[t % 2]
                hres = "hT%d" % (t % 2)
                make_hT(xt[:], xres, 0, hT, hres)
                tcols = slice(t * 128, (t + 1) * 128)
                for gi, (c0, c1) in enumerate(segs[:p1_groups]):
                    w = c1 - c0
                    Z, zres = next_ps()
                    Z = Z[:, 0:w]
                    for k in range(8):
                        P.op("tensor", "matmul", reads=[hres, "Win"], writes=[zres], out=Z, lhsT=hT[:, k, :], rhs=Win[:, k, c0:c1],
                             start=(k == 0), stop=(k == 7))
                    if gi == 0:
                        rope("vector", Z, zres, t, ro, "ro", tA, tB, "r")
                        P.op("vector", "tensor_tensor", reads=["ro", "rtab"], writes=["qb"], out=qb.rearrange("p (h d) -> p h d", h=6),
                             in0=ro.rearrange("p (h d) -> p h d", h=6), in1=bc(XI, [128, 6, 64], [2]), op=ALU.mult)
                    elif gi == 1:
                        rope("vector", Z, zres, t, ro, "ro", tA, tB, "r")
                        P.op("vector", "tensor_tensor", reads=["ro", "rtab"], writes=["kb1"], out=kb1.rearrange("p (h d) -> p h d", h=6),
                             in0=ro.rearrange("p (h d) -> p h d", h=6), in1=bc(KS1, [128, 6, 64], [2]), op=ALU.mult)
                        P.op("vector", "tensor_tensor", reads=["ro", "rtab"], writes=["kb2"], out=kb2.rearrange("p (h d) -> p h d", h=6),
                             in0=ro.rearrange("p (h d) -> p h d", h=6), in1=bc(KS2, [128, 6, 64], [2]), op=ALU.mult)
                    elif gi == 2:
                        P.op("scalar", "activation", reads=[zres], writes=["vb"], out=vb, in_=Z, func=AF.Copy)
                    elif gi == 3:
                        P.op("scalar", "activation", reads=[zres], writes=["th"], out=th[:, 0:384], in_=Z, func=AF.Tanh, scale=0.5)
                        P.op("vector", "scalar_tensor_tensor", reads=["th", zres], writes=["sg"], out=sg, in0=th[:, 0:384], scalar=1.0,
                             in1=Z, op0=ALU.add, op1=ALU.mult)
                    elif gi in (4, 5):
                        dst, dres = (aqb, "aqb") if gi == 4 else (akb, "akb")
                        P.op("scalar", "activation", reads=[zres], writes=["zc"], out=zc, in_=Z, func=AF.Copy)
                        rope("gpsimd", zc, "zc", t, dst, dres, tA2, tB2, "a")
                        for j in range(3):
                            P.op("tensor", "transpose", reads=[dres, "ident"], writes=["T1"], out=T1[:, j * 128:(j + 1) * 128],
                                 in_=dst[:, j * 128:(j + 1) * 128], identity=ident[:])
                        dT = aqT if gi == 4 else akT
                        P.op("scalar", "activation", reads=["T1"], writes=["aT%d" % gi], out=dT[:, :, tcols],
                             in_=T1[:, 0:384].rearrange("p (a b) -> p a b", a=3), func=AF.Copy)
                    elif gi == 6:
                        av_ = avb[t % 2]
                        P.op("scalar", "activation", reads=[zres], writes=["avb%d" % (t % 2)], out=av_, in_=Z, func=AF.Copy)
                        P.dma("sync", "st_v", vs_d[t * 128:(t + 1) * 128, :], av_, reads=["avb%d" % (t % 2)])
                    else:
                        P.op("scalar", "activation", reads=[zres], writes=["x2"], out=x2, in_=Z, func=AF.Square)
                        P.op("gpsimd", "tensor_scalar", reads=["x2"], writes=["inn"], out=inn, in0=x2, scalar1=0.044715, scalar2=1.0,
                             op0=ALU.mult, op1=ALU.add)
                        P.op("vector", "tensor_tensor", reads=["inn", zres], writes=["inn"], out=inn, in0=inn, in1=Z, op=ALU.mult)
                        P.op("scalar", "activation", reads=["inn"], writes=["th"], out=th, in_=inn, func=AF.Tanh,
                             scale=0.7978845608028654)
                        P.op("vector", "scalar_tensor_tensor", reads=["th", zres], writes=["gg"], out=gg, in0=th, scalar=1.0, in1=Z,
                             op0=ALU.add, op1=ALU.mult)
                        gv3 = gg[:, 256:512].rearrange("p (g d) -> p g d", g=4)
                        mean, rs = groupnorm_stats(gv3, "gg", 4, 8, 0.25, 0.5, sq[:, 0:256])
                        cen3 = cen[:, 0:256].rearrange("p (g d) -> p g d", g=4)
                        P.op("vector", "tensor_tensor", reads=["gg", "gn_mean"], writes=["cen"], out=cen3, in0=gv3,
                             in1=bc(mean, [128, 4, 64], [2]), op=ALU.subtract)
                        P.op("vector", "tensor_tensor", reads=["cen", "gn_rs"], writes=["vg"], out=vg.rearrange("p (g d) -> p g d", g=4),
                             in0=cen3, in1=bc(rs, [128, 4, 64], [2]), op=ALU.mult)
                        M, mres = next_ps()
                        for g in range(4):
                            P.op("tensor", "matmul", reads=["wsTb", "vg"], writes=[mres], out=M[:, g * 64:(g + 1) * 64], lhsT=wsTb[:, g, :],
                                 rhs=vg[:, g * 64:(g + 1) * 64], start=True, stop=True)
                        for g in range(4):
                            P.op("vector", "scalar_tensor_tensor", reads=[mres, "bsh", "gg"], writes=["ytile_gm"],
                                 out=ytile[:, 768 + g * 64:768 + (g + 1) * 64], in0=M[:, g * 64:(g + 1) * 64], scalar=bsh[:, g:g + 1],
                                 in1=gg[:, g * 64:(g + 1) * 64], op0=ALU.add, op1=ALU.mult)
                    if gi == 2:
                        for j in range(3):
                            P.op("tensor", "transpose", reads=["qb", "ident"], writes=["T1"], out=T1[:, j * 128:(j + 1) * 128],
                                 in_=qb[:, j * 128:(j + 1) * 128], identity=ident[:])
                        for j in range(3):
                            P.op("tensor", "transpose", reads=["kb1", "ident"], writes=["T1"], out=T1[:, 384 + j * 128:384 + (j + 1) * 128],
                                 in_=kb1[:, j * 128:(j + 1) * 128], identity=ident[:])
                        P.op("scalar", "activation", reads=["T1"], writes=["qkT"], out=qkT, in_=T1[:, 0:768], func=AF.Copy)
                        SC = [next_ps(), next_ps()]
                        for h in range(6):
                            j, a = h // 2, h % 2
                            sc, scres = SC[a]
                            P.op("tensor", "matmul", reads=["qkT"], writes=[scres], out=sc[:, j * 128:(j + 1) * 128],
                                 lhsT=qkT[64 * a:64 * a + 64, 384 + j * 128:384 + (j + 1) * 128],
                                 rhs=qkT[64 * a:64 * a + 64, j * 128:(j + 1) * 128], start=True, stop=True)
                        for i in range(2):
                            sc, scres = SC[i]
                            P.op("vector", "tensor_tensor", reads=[scres, "cmask"], writes=["PTr%d" % i],
                                 out=PTr.rearrange("p (j a) q -> p a j q", a=2)[:, i, :, :],
                                 in0=sc[:, 0:384].rearrange("p (a b) -> p a b", a=3), in1=bc(cmask[:], [128, 3, 128], [1]), op=ALU.mult)
                        Y, yres = next_ps()
                        for h in range(6):
                            j, a = h // 2, h % 2
                            P.op("tensor", "matmul", reads=["PTr%d" % a, "vb"], writes=[yres], out=Y[:, h * 64:(h + 1) * 64],
                                 lhsT=PTr[:, h, :], rhs=vb[:, h * 64:(h + 1) * 64], start=True, stop=False)
                            P.op("tensor", "matmul", reads=["qkT", "S_b"], writes=[yres], out=Y[:, h * 64:(h + 1) * 64],
                                 lhsT=qkT[64 * a:64 * a + 64, j * 128:(j + 1) * 128], rhs=S_b[64 * a:64 * a + 64, j, :],
                                 start=False, stop=True)
                        INC, ires = next_ps()
                        for h in range(6):
                            j, a = h // 2, h % 2
                            P.op("tensor", "matmul", reads=["kb2", "vb"], writes=[ires], out=INC[64 * a:64 * a + 64, j * 64:(j + 1) * 64],
                                 lhsT=kb2[:, h * 64:(h + 1) * 64], rhs=vb[:, h * 64:(h + 1) * 64], start=True, stop=True)
                        P.op("vector", "tensor_tensor", reads=["S_f", "rtab"], writes=["S_f"], out=S_f[:], in0=S_f[:],
                             in1=bc(DEC, [128, 3, 64], [2]), op=ALU.mult)
                        P.op("vector", "tensor_tensor", reads=["S_f", ires], writes=["S_f"], out=S_f[:], in0=S_f[:],
                             in1=INC[:, 0:192].rearrange("p (a b) -> p a b", a=3), op=ALU.add)
                        P.op("scalar", "activation", reads=["S_f"], writes=["S_b"], out=S_b[:], in_=S_f[:], func=AF.Copy)
                        ret_Y = (Y, yres)
                    if gi == 3:
                        Y, yres = ret_Y
                        Y3 = Y[:, 0:384].rearrange("p (g d) -> p g d", g=6)
                        mean, rs = groupnorm_stats(Y3, yres, 6, 8, 1.0, 0.5, sq)
                        cen3 = cen.rearrange("p (g d) -> p g d", g=6)
                        P.op("vector", "tensor_tensor", reads=[yres, "gn_mean"], writes=["cen"], out=cen3, in0=Y3,
                             in1=bc(mean, [128, 6, 64], [2]), op=ALU.subtract)
                        P.op("vector", "tensor_tensor", reads=["cen", "gn_rs"], writes=["cen"], out=cen3, in0=cen3,
                             in1=bc(rs, [128, 6, 64], [2]), op=ALU.mult)
                        P.op("vector", "tensor_tensor", reads=["cen", "sg"], writes=["ytile_ret"], out=ytile[:, 0:384], in0=cen, in1=sg,
                             op=ALU.mult)
                if p1_groups < 8:
                    continue
                for j in range(3):
                    P.op("tensor", "transpose", reads=["ytile_ret", "ident"], writes=["T1"], out=T1[:, j * 128:(j + 1) * 128],
                         in_=ytile[:, j * 128:(j + 1) * 128], identity=ident[:])
                for j in range(2):
                    P.op("tensor", "transpose", reads=["ytile_gm", "ident"], writes=["T1"], out=T1[:, 384 + j * 128:384 + (j + 1) * 128],
                         in_=ytile[:, 768 + j * 128:768 + (j + 1) * 128], identity=ident[:])
                yt_ = yTt[t % 2]
                P.op("scalar", "activation", reads=["T1"], writes=["yTt%d" % (t % 2)], out=yt_,
                     in_=T1[:, 0:640].rearrange("p (a b) -> p a b", a=5), func=AF.Copy)
                P.dma("sync", "st_y", yT_d[0:384, tcols].rearrange("(c p) n -> p c n", p=128), yt_[:, 0:3, :],
                      reads=["yTt%d" % (t % 2)])
                P.dma("sync", "st_y", yT_d[768:1024, tcols].rearrange("(c p) n -> p c n", p=128), yt_[:, 3:5, :],
                      reads=["yTt%d" % (t % 2)])
            P.barrier()
            if stop_after == (l, 1):
                break

            arW.reset()
            arK.reset()
            acc = arW.alloc([3, 2 * 2048], F32)
            assert arW.off <= 25600
            NPT = 8
            Vt = [arK.alloc([384], BF16) for _ in range(6)]
            PTa = [arK.alloc([2, 128], BF16) for _ in range(NPT)]
            yat = [arK.alloc([2048], BF16) for _ in range(2)]
            rden = arK.alloc([2048], F32)
            units = []
            for n in range(2):
                for pi, r in enumerate((1, 4, 16)):
                    bpc = 16 // r
                    for c in range(r):
                        for b in range(bpc * n, bpc * (n + 1)):
                            units.append((n, pi, r, c, b, bpc))
            hus = [(ui, j, a) for ui in range(len(units)) for j in range(3) for a in range(2)]
            SB = [(Fp[0][:], "F0"), (Fp[1][:], "F1"), (Dp[:, 0:512], "Dlo"), (Dp[:, 512:1024], "Dhi")]
            NB = [(Fp[2][:], "F2"), (Fp[3][:], "F3"), (T0[:].bitcast(F32), "T0"), (T1[:].bitcast(F32), "T1")]
            vinfo = {}
            vcnt = [0]
            ycnt = [0]

            def load_v(ui):
                n, pi, r, c, b, bpc = units[ui]
                vrows = vs_d.rearrange("(m r) d -> r m d", r=r)
                dd = {}
                for kb in ((0, 1) if b > 0 else (1,)):
                    bb = b - 1 + kb
                    vi = vcnt[0] % 6
                    vcnt[0] += 1
                    P.dma("sync", "v%d" % vi, Vt[vi], vrows[c, 128 * bb:128 * (bb + 1), :], writes=["Vt%d" % vi])
                    dd[kb] = (Vt[vi], "Vt%d" % vi)
                vinfo[ui] = dd

            def stage_S(i):
                ui, j, a = hus[i]
                n, pi, r, c, b, bpc = units[ui]
                if j == 0 and a == 0:
                    load_v(ui)
                SPt, spres = SB[i % 4]
                PT = PTa[i % NPT]
                ptres = "PTa%d" % (i % NPT)
                kbs = (0, 1) if b > 0 else (1,)
                SP3 = SPt[:, 0:256].rearrange("p (a b) -> p a b", a=2)
                qv = aqT[64 * a:64 * a + 64, j, :].rearrange("p (m r) -> p r m", r=r)[:, c, 128 * b:128 * (b + 1)]
                for kb in kbs:
                    bb = b - 1 + kb
                    kv = akT[64 * a:64 * a + 64, j, :].rearrange("p (m r) -> p r m", r=r)[:, c, 128 * bb:128 * (bb + 1)]
                    P.op("tensor", "matmul", reads=["aT4", "aT5"], writes=[spres], out=SP3[:, kb, :], lhsT=kv, rhs=qv,
                         start=True, stop=True)
                k0 = kbs[0]
                P.op("scalar", "activation", reads=[spres], writes=[ptres], out=PT[:, k0:2, :], in_=SP3[:, k0:2, :],
                     func=AF.Exp, scale=0.125)
                P.op("vector", "tensor_tensor", reads=[ptres, "amask"], writes=[ptres], out=PT[:, k0:2, :],
                     in0=PT[:, k0:2, :], in1=amask[:, k0:2, :], op=ALU.mult)

            def stage_PV(i):
                ui, j, a = hus[i]
                n, pi, r, c, b, bpc = units[ui]
                NPt, npres = NB[(i // 2) % 4]
                PT = PTa[i % NPT]
                ptres = "PTa%d" % (i % NPT)
                kbs = (0, 1) if b > 0 else (1,)
                h = 2 * j + a
                vbuf = vinfo[ui]
                for which in range(2):
                    for kb in kbs:
                        if which == 0:
                            lhsT = vbuf[kb][0][:, h * 64:(h + 1) * 64]
                            rd = [vbuf[kb][1], ptres]
                        else:
                            lhsT = ones[:]
                            rd = ["ones", ptres]
                        P.op("tensor", "matmul", reads=rd, writes=[npres],
                             out=NPt[64 * a:64 * a + 64, which * 128:(which + 1) * 128], lhsT=lhsT, rhs=PT[:, kb, :],
                             start=(kb == kbs[0]), stop=(kb == kbs[-1]))
                if a == 1:
                    av = acc[:, j, :].rearrange("p (w m r) -> p w r m", w=2, r=r)[:, :, c,
                                                                                 128 * (b - bpc * n):128 * (b - bpc * n + 1)]
                    np3 = NPt[:, 0:256].rearrange("p (w q) -> p w q", w=2)
                    if pi == 0:
                        P.op("vector", "tensor_copy", reads=[npres], writes=["acc%d" % j], out=av, in_=np3)
                    else:
                        P.op("vector", "tensor_tensor", reads=[npres, "acc%d" % j], writes=["acc%d" % j], out=av, in0=np3, in1=av,
                             op=ALU.add)

            def normalise(n):
                for j in range(3):
                    yb = yat[ycnt[0] % 2]
                    yres = "yat%d" % (ycnt[0] % 2)
                    ycnt[0] += 1
                    P.op("vector", "reciprocal", reads=["acc%d" % j], writes=["rden"], out=rden, in_=acc[:, j, 2048:4096])
                    P.op("gpsimd", "tensor_tensor", reads=["acc%d" % j, "rden"], writes=[yres], out=yb, in0=acc[:, j, 0:2048], in1=rden,
                         op=ALU.mult)
                    P.dma("sync", "st_y", yT_d[384 + 128 * j:384 + 128 * (j + 1), 2048 * n:2048 * (n + 1)], yb, reads=[yres])

            DEPTH = 3
            NH = len(hus)
            for i in range(NH + DEPTH):
                if i < NH:
                    stage_S(i)
                if i >= DEPTH:
                    stage_PV(i - DEPTH)
                    if (i - DEPTH + 1) % (NH // 2) == 0:
                        normalise((i - DEPTH) // (NH // 2))
            P.barrier()
            if stop_after == (l, 2):
                break

            arW.reset()
            arK.reset()
            Wout = arW.alloc([8, D], BF16)
            ptmp = arK.alloc([D], F32)
            yTl = [arK.alloc([8, 128], BF16), arK.alloc([8, 128], BF16)]
            for c in range(8):
                P.dma("gpsimd", "wout", Wout[:, c, :], w_out_d[l, c * 128:(c + 1) * 128, :], writes=["Wout"])
            P.dma("sync", "c3", gpost[:], gpost_in[l, 0], writes=["gpost"])
            x_dst = xs_d

            def p3_load(t):
                P.dma("sync", "x%d" % (t % 2), xb[t % 2][:], x_src[t * 128:(t + 1) * 128, :], writes=["xb%d" % (t % 2)])
                P.dma("sync", "yl%d" % (t % 2), yTl[t % 2], yT_d[:, t * 128:(t + 1) * 128].rearrange("(c p) n -> p c n", p=128),
                      writes=["yTl%d" % (t % 2)])
            p3_load(0)
            for t in range(NT):
                if t + 1 < NT:
                    p3_load(t + 1)
                xt = xb[t % 2]
                xres = "xb%d" % (t % 2)
                for hf in range(2):
                    for c in range(8):
                        P.op("tensor", "matmul", reads=["yTl%d" % (t % 2), "Wout"], writes=["Dp"], out=Dp[:, hf * 512:(hf + 1) * 512],
                             lhsT=yTl[t % 2][:, c, :], rhs=Wout[:, c, hf * 512:(hf + 1) * 512], start=(c == 0), stop=(c == 7))
                post_norm_residual(xt[:], xres)
                P.dma("sync", "st_x", x_dst[t * 128:(t + 1) * 128, :], xt[:], reads=[xres])
            x_src = xs_d
            P.barrier()
            if stop_after == (l, 3):
                break

            arW.reset()
            arK.reset()
            Wgu = arW.alloc([8, 2 * FH], BF16)
            Wdn = arW.alloc([22, D], BF16)
            xn = arK.alloc([D], BF16)
            ptmp = arK.alloc([D], F32)
            hTb = [arK.alloc([8, 128], BF16), arK.alloc([8, 128], BF16)]
            act = arK.alloc([FH], BF16)
            actT = arK.alloc([22, 128], BF16)
            thb = [arK.alloc([512], F32), arK.alloc([512], F32)]
            ub = [arK.alloc([512], F32), arK.alloc([512], F32)]
            for c in range(8):
                P.dma("gpsimd", "wgu", Wgu[:, c, :], w_gu_d[l, c * 128:(c + 1) * 128, :], writes=["Wgu"])
            for i in range(2):
                P.dma("gpsimd", "wdn", Wdn[:, 11 * i:11 * (i + 1), :],
                      w_dn_d[l, 11 * 128 * i:11 * 128 * (i + 1), :].rearrange("(c p) n -> p c n", p=128), writes=["Wdn"])
            P.dma("sync", "c3", gpost[:], gpost_in[l, 1], writes=["gpost"])
            x_dst = out_d if l == n_layers - 1 else xs_d
            pieces = [(i * 512, 512) for i in range(5)] + [(2560, 256)]
            P.dma("sync", "x0", xb[0][:], x_src[0:128, :], writes=["xb0"])
            pc = 0
            if p4_tiles == 0:
                P.op("tensor", "matmul", reads=["Wgu", "Wdn"], writes=["Dp"], out=Dp[:, 0:512], lhsT=Wgu[:, 0, 0:128], rhs=Wdn[:, 0, 0:512],
                     start=True, stop=True)

            def act_T(pi_):
                cs = list(range(4 * pi_, min(22, 4 * pi_ + 4)))
                for i, c in enumerate(cs):
                    P.op("tensor", "transpose", reads=["act%d" % pi_, "ident"], writes=["T1"], out=T1[:, i * 128:(i + 1) * 128],
                         in_=act[:, c * 128:(c + 1) * 128], identity=ident[:])
                P.op("scalar", "activation", reads=["T1"], writes=["actT"], out=actT[:, cs[0]:cs[-1] + 1, :],
                     in_=T1[:, 0:128 * len(cs)].rearrange("p (a b) -> p a b", a=len(cs)), func=AF.Copy)

            if p4_tiles > 0:
                make_xn(xb[0][:], "xb0")
                make_hT2(1, hTb[0], "hT0")
            for t in range(p4_tiles):
                if t + 1 < NT:
                    P.dma("sync", "x%d" % ((t + 1) % 2), xb[(t + 1) % 2][:], x_src[(t + 1) * 128:(t + 2) * 128, :],
                          writes=["xb%d" % ((t + 1) % 2)])
                xt = xb[t % 2]
                xres = "xb%d" % (t % 2)
                hT = hTb[t % 2]
                hres = "hT%d" % (t % 2)
                for pi_, (p0, pw) in enumerate(pieces):
                    G, gres = (Fp[0][:], "F0") if pc % 2 == 0 else (Fp[2][:], "F2")
                    U, ures = (Fp[1][:], "F1") if pc % 2 == 0 else (Fp[3][:], "F3")
                    th_, thres = thb[pc % 2], "thb%d" % (pc % 2)
                    u_, ures2 = ub[pc % 2], "ub%d" % (pc % 2)
                    pc += 1
                    for k in range(8):
                        P.op("tensor", "matmul", reads=[hres, "Wgu"], writes=[gres], out=G[:, 0:pw], lhsT=hT[:, k, :],
                             rhs=Wgu[:, k, p0:p0 + pw], start=(k == 0), stop=(k == 7))
                    for k in range(8):
                        P.op("tensor", "matmul", reads=[hres, "Wgu"], writes=[ures], out=U[:, 0:pw], lhsT=hT[:, k, :],
                             rhs=Wgu[:, k, FH + p0:FH + p0 + pw], start=(k == 0), stop=(k == 7))
                    P.op("scalar", "activation", reads=[gres], writes=[thres], out=th_[:, 0:pw], in_=G[:, 0:pw], func=AF.Tanh, scale=0.5)
                    P.op("vector", "scalar_tensor_tensor", reads=[thres, gres], writes=[ures2], out=u_[:, 0:pw], in0=th_[:, 0:pw],
                         scalar=1.0, in1=G[:, 0:pw], op0=ALU.add, op1=ALU.mult)
                    P.op("vector", "scalar_tensor_tensor", reads=[ures2, ures], writes=["act%d" % pi_], out=act[:, p0:p0 + pw], in0=u_[:, 0:pw],
                         scalar=0.5, in1=U[:, 0:pw], op0=ALU.mult, op1=ALU.mult)
                    if pi_ >= 1:
                        act_T(pi_ - 1)
                    if pi_ == 3 and t + 1 < p4_tiles:
                        make_xn(xb[(t + 1) % 2][:], "xb%d" % ((t + 1) % 2))
                if t + 1 < p4_tiles:
                    make_hT2(1, hTb[(t + 1) % 2], "hT%d" % ((t + 1) % 2))
                act_T(5)
                for hf in range(2):
                    for c in range(22):
                        P.op("tensor", "matmul", reads=["actT", "Wdn"], writes=["Dp"], out=Dp[:, hf * 512:(hf + 1) * 512],
                             lhsT=actT[:, c, :], rhs=Wdn[:, c, hf * 512:(hf + 1) * 512], start=(c == 0), stop=(c == 21))
                post_norm_residual(xt[:], xres)
                P.dma("sync", "st_x", x_dst[t * 128:(t + 1) * 128, :], xt[:], reads=[xres])
            x_src = xs_d
            P.barrier()
        P.emit()
    return nc


def _consts():
    ident = np.eye(128, dtype=np.float32).astype(ml_dtypes.bfloat16)
    k = np.arange(128)[:, None]
    q = np.arange(128)[None, :]
    prev = (k >= q).astype(np.float32)
    cur = (k <= q).astype(np.float32)
    amask = np.concatenate([prev, cur], axis=1).astype(ml_dtypes.bfloat16)
    cmask = cur.astype(ml_dtypes.bfloat16)
    inv_freq = (10000.0 ** (-np.arange(0, 64, 2, dtype=np.float32) / np.float32(64))).astype(np.float32)
    invf = np.broadcast_to(inv_freq[None, :], (128, 32)).copy()
    h = np.arange(6, dtype=np.float64)
    lg = np.log(1.0 - 2.0 ** (-5.0 - h))
    p = np.arange(128, dtype=np.float64)[:, None]
    xi = np.exp(lg[None, :] * (p + 1.0))
    ks1 = np.exp(-lg[None, :] * (p + 1.0)) / 8.0
    ks2 = np.exp(lg[None, :] * (127.0 - p)) / 8.0
    dec = np.zeros((128, 3))
    for j in range(3):
        dec[0:64, j] = np.exp(lg[2 * j] * 128.0)
        dec[64:128, j] = np.exp(lg[2 * j + 1] * 128.0)
    rtab = np.concatenate([xi, ks1, ks2, dec], axis=1).astype(np.float32)
    return dict(ident=ident, amask=amask, cmask=cmask, invf=invf, rtab=rtab)


def make_in_maps(x, positions, g_mix_pre, g_mix_post, w_in, w_s, b_s, w_out, g_ffn_pre, g_ffn_post, w_gate_up, w_down):
    f32 = np.float32
    c = _consts()
    gpre = np.stack([np.asarray(g_mix_pre, f32), np.asarray(g_ffn_pre, f32)], axis=1)
    gpre = np.ascontiguousarray(gpre.reshape(L, 2, 8, 128).transpose(0, 1, 3, 2))
    gpost = np.stack([np.asarray(g_mix_post, f32), np.asarray(g_ffn_post, f32)], axis=1)
    gpost = np.ascontiguousarray(np.broadcast_to(gpost[:, :, None, :], (L, 2, 128, D)))
    wsT = np.ascontiguousarray(np.asarray(w_s, f32).transpose(0, 3, 1, 2))
    bsT = np.ascontiguousarray(np.asarray(b_s, f32).transpose(0, 2, 1))
    shared = dict(gpre=gpre, gpost=gpost, w_in=np.ascontiguousarray(w_in, dtype=f32), wsT=wsT, bsT=bsT,
                  w_out=np.ascontiguousarray(w_out, dtype=f32), w_gu=np.ascontiguousarray(w_gate_up, dtype=f32),
                  w_dn=np.ascontiguousarray(w_down, dtype=f32), **c)
    maps = []
    for b in range(x.shape[0]):
        pos = np.ascontiguousarray(np.asarray(positions[b], np.int32).reshape(NT, 128).T)
        m = dict(shared)
        m["x"] = np.ascontiguousarray(x[b], dtype=f32)
        m["pos"] = pos
        maps.append(m)
    return maps


_NC_CACHE = {}


def kernel(x, positions, g_mix_pre, g_mix_post, w_in, w_s, b_s, w_out, g_ffn_pre, g_ffn_post, w_gate_up, w_down):
    x = np.asarray(x)
    B = x.shape[0]
    if "nc" not in _NC_CACHE:
        _NC_CACHE["nc"] = build_nc()
    nc = _NC_CACHE["nc"]
    maps = make_in_maps(x, np.asarray(positions), g_mix_pre, g_mix_post, w_in, w_s, b_s, w_out, g_ffn_pre, g_ffn_post,
                        w_gate_up, w_down)
    res = run_bass_kernel_spmd(nc, maps, core_ids=list(range(B)))
    out = np.stack([np.asarray(r["out"], dtype=np.float32) for r in res.results], axis=0)
    return out
```
